# Optimizing a Trainium2 kernel written in Bass

```python
import functools
import jax, jax.numpy as jnp
from jax import lax
import numpy as np

D_MODEL = 2048
BATCH = 2
SEQ = 4096
DEPTH = 4
DEC_BATCH = 32
DEC_SEQ = 8
PAST_LEN = 16384
PAGE_SIZE = 128

N_META = 16
N_HEADS = 16
N_KV_HEADS = 4
HEAD_DIM = 64
GROUP = N_HEADS // N_KV_HEADS
WINDOW = 128
BLOCK = 128
ROPE_DIM = HEAD_DIM // 4
ROPE_THETA = 500000.0
CONV_CH = 1024
CONV_W = 31
D_FF = -(-8 * D_MODEL // (3 * 256)) * 256
Q_W = N_HEADS * HEAD_DIM
KV_W = N_KV_HEADS * HEAD_DIM
IN_COLS = Q_W + 2 * KV_W + 2 * CONV_CH + 2 * D_MODEL
EPS = 1e-6
NEG = -1e30

kernel_name = "gated_swa_conformer_hybrid_step"


def rms_norm(x, g):
    xf = x.astype(jnp.float32)
    y = xf * lax.rsqrt(jnp.mean(xf * xf, axis=-1, keepdims=True) + EPS)
    return (y * g.astype(jnp.float32)).astype(x.dtype)


def layer_norm(x, g, b):
    xf = x.astype(jnp.float32)
    mu = jnp.mean(xf, axis=-1, keepdims=True)
    var = jnp.mean(jnp.square(xf - mu), axis=-1, keepdims=True)
    y = (xf - mu) * lax.rsqrt(var + EPS)
    return (y * g.astype(jnp.float32) + b.astype(jnp.float32)).astype(x.dtype)


def rotary(x, pos):
    half = ROPE_DIM // 2
    inv = ROPE_THETA ** (-jnp.arange(half, dtype=jnp.float32) * 2.0 / ROPE_DIM)
    ang = pos.astype(jnp.float32)[:, None] * inv[None, :]
    cos = jnp.cos(ang)[:, None, :].astype(x.dtype)
    sin = jnp.sin(ang)[:, None, :].astype(x.dtype)
    x1 = x[..., :half]
    x2 = x[..., half:ROPE_DIM]
    return jnp.concatenate([x1 * cos - x2 * sin, x2 * cos + x1 * sin, x[..., ROPE_DIM:]], axis=-1)


def sink_attention(q, k, v, q_pos, k_pos, sinks):
    s = jnp.einsum('bnqhgd,bnkhd->bnhgqk', q, k).astype(jnp.float32) * (HEAD_DIM ** -0.5)
    diff = q_pos[:, :, None] - k_pos[:, None, :]
    mask = (diff >= 0) & (diff <= WINDOW) & (k_pos[:, None, :] >= 0)
    s = jnp.where(mask[None, :, None, None], s, NEG)
    sink = jnp.broadcast_to(sinks.astype(jnp.float32).reshape(N_KV_HEADS, GROUP)[None, None, :, :, None, None],
                            s.shape[:-1] + (1,))
    p = jax.nn.softmax(jnp.concatenate([s, sink], axis=-1), axis=-1)[..., :-1]
    return jnp.einsum('bnhgqk,bnkhd->bnqhgd', p.astype(v.dtype), v)


def attend_prompt(q, k, v, sinks):
    B, L = q.shape[0], q.shape[1]
    pad = (-N_META) % BLOCK
    nb = (L + pad) // BLOCK
    padf = lambda a: jnp.pad(a, ((0, 0), (pad, 0)) + ((0, 0),) * (a.ndim - 2))
    shift = lambda a: jnp.concatenate([jnp.zeros_like(a[:, :1]), a[:, :-1]], axis=1)
    qb = padf(q).reshape(B, nb, BLOCK, N_KV_HEADS, GROUP, HEAD_DIM)
    kb = padf(k).reshape(B, nb, BLOCK, N_KV_HEADS, HEAD_DIM)
    vb = padf(v).reshape(B, nb, BLOCK, N_KV_HEADS, HEAD_DIM)
    k2 = jnp.concatenate([shift(kb), kb], axis=2)
    v2 = jnp.concatenate([shift(vb), vb], axis=2)
    pos = jnp.concatenate([jnp.full((pad,), -1, jnp.int32), jnp.arange(L, dtype=jnp.int32)]).reshape(nb, BLOCK)
    pos_prev = jnp.concatenate([jnp.full((1, BLOCK), -1, jnp.int32), pos[:-1]], axis=0)
    kpos = jnp.concatenate([pos_prev, pos], axis=1)
    o = sink_attention(qb, k2, v2, pos, kpos, sinks)
    o = o.reshape(B, nb * BLOCK, Q_W)[:, pad:]
    return o, k[:, -WINDOW:], v[:, -WINDOW:]


def attend_sample(q, k, v, sinks, past_k, past_v):
    DB, T = q.shape[0], q.shape[1]
    kk = jnp.concatenate([past_k, k], axis=1)
    vv = jnp.concatenate([past_v, v], axis=1)
    qpos = PAST_LEN + jnp.arange(T, dtype=jnp.int32)
    kpos = jnp.concatenate([PAST_LEN - WINDOW + jnp.arange(WINDOW, dtype=jnp.int32), qpos])
    o = sink_attention(q.reshape(DB, 1, T, N_KV_HEADS, GROUP, HEAD_DIM), kk[:, None], vv[:, None],
                       qpos[None], kpos[None], sinks)
    return o.reshape(DB, T, Q_W), kk[:, -WINDOW:], vv[:, -WINDOW:]


def conv_module(a_glu, past, w_dw, b_dw, ln_g, ln_b, w_out, b_out):
    u = a_glu[..., :CONV_CH] * jax.nn.sigmoid(a_glu[..., CONV_CH:])
    ext = jnp.concatenate([past.astype(u.dtype), u], axis=1)
    y = lax.conv_general_dilated(ext, w_dw[:, None, :].astype(u.dtype), window_strides=(1,), padding='VALID',
                                 dimension_numbers=('NWC', 'WIO', 'NWC'),
                                 feature_group_count=CONV_CH) + b_dw
    y = jax.nn.silu(layer_norm(y, ln_g, ln_b))
    return y @ w_out + b_out, ext[:, -(CONV_W - 1):]


def trunk_layer(x, pos, attend, conv_past, w_in, w_attn_out, w_dw, b_dw, ln_g, ln_b,
                w_conv_out, b_conv_out, w_o, g_mix, g_ffn, w_ffn_in, w_ffn_out):
    B, T = x.shape[0], x.shape[1]
    h = rms_norm(x, g_mix)
    u = h @ w_in
    o1 = Q_W
    o2 = o1 + KV_W
    o3 = o2 + KV_W
    o4 = o3 + 2 * CONV_CH
    o5 = o4 + D_MODEL
    q = rotary(u[..., :o1].reshape(B, T, N_HEADS, HEAD_DIM), pos)
    k = rotary(u[..., o1:o2].reshape(B, T, N_KV_HEADS, HEAD_DIM), pos)
    v = u[..., o2:o3].reshape(B, T, N_KV_HEADS, HEAD_DIM)
    a_glu = u[..., o3:o4]
    g_a = u[..., o4:o5]
    g_b = u[..., o5:]
    o_attn, new_k, new_v = attend(q, k, v)
    br_a = o_attn @ w_attn_out
    br_b, new_conv = conv_module(a_glu, conv_past, w_dw, b_dw, ln_g, ln_b, w_conv_out, b_conv_out)
    x = x + (jax.nn.sigmoid(g_a) * br_a + jax.nn.sigmoid(g_b) * br_b) @ w_o
    h = rms_norm(x, g_ffn)
    gu = h @ w_ffn_in
    x = x + (jax.nn.silu(gu[..., :D_FF]) * gu[..., D_FF:]) @ w_ffn_out
    return x, new_k, new_v, new_conv


def setup_inputs(seed: int = 0) -> dict:
    key = jax.random.key(seed)
    ks = jax.random.split(key, 24)
    nrm = lambda k, shape, s: jax.random.normal(k, shape, jnp.float32) * s
    return {
        "x_prompt": nrm(ks[0], (BATCH, SEQ, D_MODEL), 1.0),
        "x_sample": nrm(ks[1], (DEC_BATCH, DEC_SEQ, D_MODEL), 1.0),
        "cache_k": nrm(ks[2], (DEPTH, DEC_BATCH, WINDOW, N_KV_HEADS, HEAD_DIM), 1.0),
        "cache_v": nrm(ks[3], (DEPTH, DEC_BATCH, WINDOW, N_KV_HEADS, HEAD_DIM), 1.0),
        "state_conv": nrm(ks[4], (DEPTH, DEC_BATCH, CONV_W - 1, CONV_CH), 0.5),
        "meta_tokens": nrm(ks[5], (N_META, D_MODEL), 1.0),
        "w_in": nrm(ks[6], (DEPTH, D_MODEL, IN_COLS), D_MODEL ** -0.5),
        "attn_sinks": nrm(ks[7], (DEPTH, N_HEADS), 0.5),
        "w_attn_out": nrm(ks[8], (DEPTH, Q_W, D_MODEL), Q_W ** -0.5),
        "conv_dw": nrm(ks[9], (DEPTH, CONV_W, CONV_CH), CONV_W ** -0.5),
        "conv_dw_bias": nrm(ks[10], (DEPTH, CONV_CH), 0.02),
        "conv_ln_g": 1.0 + nrm(ks[11], (DEPTH, CONV_CH), 0.02),
        "conv_ln_b": nrm(ks[12], (DEPTH, CONV_CH), 0.02),
        "w_conv_out": nrm(ks[13], (DEPTH, CONV_CH, D_MODEL), CONV_CH ** -0.5),
        "b_conv_out": nrm(ks[14], (DEPTH, D_MODEL), 0.02),
        "w_o": nrm(ks[15], (DEPTH, D_MODEL, D_MODEL), D_MODEL ** -0.5),
        "norm_mix": 1.0 + nrm(ks[16], (DEPTH, D_MODEL), 0.02),
        "norm_ffn": 1.0 + nrm(ks[17], (DEPTH, D_MODEL), 0.02),
        "w_ffn_in": nrm(ks[18], (DEPTH, D_MODEL, 2 * D_FF), D_MODEL ** -0.5),
        "w_ffn_out": nrm(ks[19], (DEPTH, D_FF, D_MODEL), D_FF ** -0.5),
        "norm_final": 1.0 + nrm(ks[20], (D_MODEL,), 0.02),
    }


def reference(x_prompt, x_sample, cache_k, cache_v, state_conv, meta_tokens, w_in, attn_sinks, w_attn_out,
              conv_dw, conv_dw_bias, conv_ln_g, conv_ln_b, w_conv_out, b_conv_out, w_o, norm_mix, norm_ffn,
              w_ffn_in, w_ffn_out, norm_final):
    B = x_prompt.shape[0]
    DB, T = x_sample.shape[0], x_sample.shape[1]
    meta = jnp.broadcast_to(meta_tokens.astype(x_prompt.dtype)[None], (B, N_META, D_MODEL))
    xp = jnp.concatenate([meta, x_prompt], axis=1)
    L = xp.shape[1]
    pos_p = jnp.arange(L, dtype=jnp.int32)
    pos_s = PAST_LEN + jnp.arange(T, dtype=jnp.int32)
    xs = x_sample
    conv_zero = jnp.zeros((B, CONV_W - 1, CONV_CH), x_prompt.dtype)
    nk_p, nv_p, nc_p, nk_s, nv_s, nc_s = [], [], [], [], [], []
    for l in range(DEPTH):
        w = (w_in[l], w_attn_out[l], conv_dw[l], conv_dw_bias[l], conv_ln_g[l], conv_ln_b[l],
             w_conv_out[l], b_conv_out[l], w_o[l], norm_mix[l], norm_ffn[l], w_ffn_in[l], w_ffn_out[l])
        att_p = functools.partial(attend_prompt, sinks=attn_sinks[l])
        att_s = functools.partial(attend_sample, sinks=attn_sinks[l], past_k=cache_k[l], past_v=cache_v[l])
        xp, kp, vp, cp = trunk_layer(xp, pos_p, att_p, conv_zero, *w)
        xs, ks_, vs_, cs_ = trunk_layer(xs, pos_s, att_s, state_conv[l], *w)
        nk_p.append(kp)
        nv_p.append(vp)
        nc_p.append(cp)
        nk_s.append(ks_)
        nv_s.append(vs_)
        nc_s.append(cs_)
    y_prompt = rms_norm(xp, norm_final)[:, N_META:]
    y_sample = rms_norm(xs, norm_final)
    new_k_prompt = jnp.stack(nk_p)
    new_v_prompt = jnp.stack(nv_p)
    new_conv_prompt = jnp.stack(nc_p)
    new_k_sample = jnp.stack(nk_s)
    new_v_sample = jnp.stack(nv_s)
    new_conv_sample = jnp.stack(nc_s)
    return (y_prompt, y_sample, new_k_prompt, new_v_prompt, new_conv_prompt, new_k_sample, new_v_sample, new_conv_sample)
```

```python
import contextlib
import numpy as np
import concourse.bass as bass
import concourse.mybir as mybir
from concourse.bass_utils import run_bass_kernel_spmd

F32 = mybir.dt.float32
BF16 = mybir.dt.bfloat16
ALU = mybir.AluOpType
AF = mybir.ActivationFunctionType
AX = mybir.AxisListType

D = 2048
NCH = 16
DEPTH = 4
SEQ = 4096
NSEQ_S = 4
TS = 8
NEG = -30000.0
EPS = 1e-6
SCALE = 0.125
PV_L = 320
PV_FIN = DEPTH * PV_L
PV_SINK = PV_FIN + 16
NV = PV_SINK + DEPTH * 16
NWSLOT = 5
NBLK_L = 262
USE_SCRATCH = False
NTMP = 4


class Tok:
    __slots__ = ("sem", "val", "eng")

    def __init__(self, eng):
        self.sem = None
        self.val = None
        self.eng = eng


class Sched:
    ENGS = ("pe", "act", "dve", "pool", "sp")
    EPOCH = 20000

    def __init__(self, nc):
        self.nc = nc
        self.ops = {e: [] for e in self.ENGS}
        self.cnt = {}
        self.last_w = {}
        self.readers = {}
        self.pending = {e: [] for e in self.ENGS}
        self.nsig = {e: 0 for e in self.ENGS}
        self.last_dma = {}
        self.out_toks = []

    @staticmethod
    def _is_psum(k):
        return (isinstance(k, tuple) and k[0] == "pj") or k in ("pS", "pTO")

    def op(self, eng, fn, reads=(), writes=(), dma=None, signal=True, extra=()):
        writes = list(writes) + [r for r in reads if self._is_psum(r) and r not in writes]
        deps = []
        for r in reads:
            t = self.last_w.get(r)
            if t is not None:
                deps.append(t)
        for w in writes:
            t = self.last_w.get(w)
            if t is not None:
                deps.append(t)
            deps.extend(self.readers.get(w, ()))
        deps.extend(extra)
        tok = Tok(eng if dma is None else "dma")
        if dma is not None:
            t = self.last_dma.get(dma)
            if t is not None:
                deps.append(t)
            tok.sem = "dma_" + dma
            self.cnt[tok.sem] = self.cnt.get(tok.sem, 0) + 16
            tok.val = self.cnt[tok.sem]
            self.last_dma[dma] = tok
            inc = (tok.sem, 16)
        elif signal:
            ep = self.nsig[eng] // self.EPOCH
            self.nsig[eng] += 1
            tok.sem = "%s_%d" % (eng, ep)
            self.cnt[tok.sem] = self.cnt.get(tok.sem, 0) + 1
            tok.val = self.cnt[tok.sem]
            for p in self.pending[eng]:
                p.sem, p.val = tok.sem, tok.val
            self.pending[eng] = []
            inc = (tok.sem, 1)
        else:
            self.pending[eng].append(tok)
            inc = None
        waits = []
        seen = set()
        for t in deps:
            if id(t) in seen:
                continue
            seen.add(id(t))
            if eng in ("pe", "act") and t.eng == eng:
                continue
            waits.append(t)
        self.ops[eng].append((fn, waits, inc))
        for r in reads:
            self.readers.setdefault(r, []).append(tok)
        for w in writes:
            self.last_w[w] = tok
            self.readers[w] = []
        return tok

    def build(self):
        nc = self.nc
        for e in self.ENGS:
            assert not self.pending[e], "unsignalled tail ops on " + e
        sems = {name: nc.alloc_semaphore(name) for name in sorted(self.cnt)}
        eng_attr = {"pe": "tensor", "act": "scalar", "dve": "vector", "pool": "gpsimd", "sp": "sync"}
        with nc.Block() as block:
            for e in self.ENGS:
                ops = self.ops[e]

                def body(engine, ops=ops):
                    waited = {}
                    for fn, waits, inc in ops:
                        for t in waits:
                            assert t.sem is not None
                            if waited.get(t.sem, 0) >= t.val:
                                continue
                            waited[t.sem] = t.val
                            engine.wait_ge(sems[t.sem], t.val)
                        if fn is None:
                            continue
                        ins = fn(engine)
                        if inc is not None:
                            ins.then_inc(sems[inc[0]], inc[1])

                getattr(block, eng_attr[e])(body)


class Prog:
    def __init__(self, depth=DEPTH, ngroups=3):
        self.depth = depth
        self.ngroups = ngroups
        self.nc = bass.Bass("TRN2", target_bir_lowering=False)
        self.S = Sched(self.nc)
        self.es = contextlib.ExitStack()
        self.rot = {}

    def dram(self, name, shape, kind):
        return self.nc.dram_tensor(name, list(shape), F32, kind=kind).ap()

    def sb(self, name, shape, dt):
        return self.es.enter_context(self.nc.sbuf_tensor("s_" + name, list(shape), dt))

    def ps(self, name, shape, dt):
        return self.es.enter_context(self.nc.psum_tensor("p_" + name, list(shape), dt))

    def nxt(self, what, n):
        i = self.rot.get(what, 0)
        self.rot[what] = i + 1
        return i % n

    def pj(self):
        i = self.nxt("pj", 6)
        return self.pjb[i], ("pj", i)

    def tmp(self):
        i = self.nxt("tmp", NTMP)
        return self.tmps[i], ("t", i)

    def PE(self, fn, r=(), w=(), sig=True):
        return self.S.op("pe", fn, r, w, signal=sig)

    def ACT(self, fn, r=(), w=()):
        return self.S.op("act", fn, r, w)

    def DVE(self, fn, r=(), w=()):
        return self.S.op("dve", fn, r, w)

    def act_copy(self, out, in_, r, w, func=AF.Copy, **kw):
        return self.ACT(lambda e: e.activation(out=out, in_=in_, func=func, **kw), r, w)

    def dve_copy(self, out, in_, r, w):
        return self.DVE(lambda e: e.tensor_copy(out=out, in_=in_), r, w)

    def tt(self, out, in0, in1, op, r, w):
        return self.DVE(lambda e: e.tensor_tensor(out=out, in0=in0, in1=in1, op=op), r, w)

    def ts(self, out, in0, s1, s2, op0, op1, r, w):
        if s2 is None:
            return self.DVE(lambda e: e.tensor_scalar(out=out, in0=in0, scalar1=s1, scalar2=None, op0=op0), r, w)
        return self.DVE(lambda e: e.tensor_scalar(out=out, in0=in0, scalar1=s1, scalar2=s2, op0=op0, op1=op1), r, w)

    def stt(self, out, in0, sc, in1, op0, op1, r, w):
        return self.DVE(lambda e: e.scalar_tensor_tensor(out=out, in0=in0, scalar=sc, in1=in1, op0=op0, op1=op1), r, w)

    def mm(self, out, lhsT, rhs, start, stop, r, w, sig=None):
        if DBG == "dmaonly":
            return None
        return self.PE(lambda e: e.matmul(out, lhsT=lhsT, rhs=rhs, start=start, stop=stop), r, w,
                       sig=(stop if sig is None else sig))

    def tr(self, out, in_, ident, r, w, sig):
        return self.PE(lambda e: e.transpose(out=out, in_=in_, identity=ident), r, w, sig=sig)

    def dma_in(self, out, in_, w, key, r=()):
        return self.S.op("sp", lambda e: e.dma_start(out=out, in_=in_), r, w, dma=key)

    def dma_out(self, out, in_, r):
        key = "out%d" % self.nxt("outk", 8)
        t = self.S.op("sp", lambda e: e.dma_start(out=out, in_=in_), r, (), dma=key)
        self.S.out_toks.append(t)
        return t

    def load_w(self, src, nk, ncols):
        i = self.nxt("w", NWSLOT)
        slot = self.wsl[i]
        n = nk * ncols
        dst = slot[:, 0:n]
        bid = self.bctr
        self.bctr += 1
        sc = self.wscs[bid // NBLK_L][bid % NBLK_L, :, 0:n]
        if self.cur_g == 0 or not USE_SCRATCH:
            self.S.op("pool", lambda e: e.dma_start(out=dst, in_=src), (), [("w", i)], dma="w%d" % i)
            if USE_SCRATCH and self.ngroups > 1:
                self.S.op("sp", lambda e: e.dma_start(out=sc, in_=dst), [("w", i)], [("wsc", bid)], dma="wb%d" % i)
        else:
            self.S.op("pool", lambda e: e.dma_start(out=dst, in_=sc), [("wsc", bid)], [("w", i)], dma="w%d" % i)
        return dst.rearrange("p (k n) -> p k n", k=nk), ("w", i)

    def rsqrt_inplace(self, ap, key, src, srckey):
        self.ts(ap, src, EPS, None, ALU.add, None, [srckey], [key])
        self.act_copy(ap, ap, [key], [key], func=AF.Sqrt)
        self.DVE(lambda e: e.reciprocal(out=ap, in_=ap), [key], [key])

    def build(self):
        with self.es:
            self._alloc()
            self._consts()
            for g in range(self.ngroups):
                if STOP == "consts":
                    break
                self.group(g)
            self.S.op("sp", None, extra=list(self.S.out_toks), signal=False)
            self.S.pending["sp"] = []
            self.S.build()
        return self.nc

    def _alloc(self):
        dp = self.depth
        I, O = "ExternalInput", "ExternalOutput"
        self.xin = self.dram("xin", [12 * 128 + 32, D], I)
        self.ck = self.dram("ck", [DEPTH, NSEQ_S, 128, 256], I)
        self.cv = self.dram("cv", [DEPTH, NSEQ_S, 128, 256], I)
        self.sc = self.dram("sc", [DEPTH, NSEQ_S, 30, 1024], I)
        self.pvec_d = self.dram("pvec", [128, NV], I)
        self.rope_d = self.dram("rope", [3, 2, 128, 544], I)
        self.masks_d = self.dram("masks", [3, 128, 256], I)
        self.valid_d = self.dram("valid", [128, 512], I)
        self.ident_d = self.dram("identf", [128, 128], I)
        self.rm_d = self.dram("rmat", [128, 128], I)
        self.win = self.dram("win_r", [dp, 62, 128, 16 * 128], I)
        self.wao = self.dram("wao_r", [dp, 16, 128, 8 * 128], I)
        self.wco = self.dram("wco_r", [dp, 16, 128, 8 * 128], I)
        self.wo = self.dram("wo_r", [dp, 16, 128, 16 * 128], I)
        self.wf1 = self.dram("wf1_r", [dp, 88, 128, 16 * 128], I)
        self.wf2 = self.dram("wf2_r", [dp, 4, 16, 128, 11 * 128], I)
        self.wscs = [self.nc.dram_tensor("wsc%d" % l, [NBLK_L, 128, 2048], BF16, kind="Internal").ap() for l in range(dp)]
        self.y = self.dram("y", [1024 + 32, D], O)
        self.klast = self.dram("klast", [DEPTH, 128, 256], O)
        self.vlast = self.dram("vlast", [DEPTH, 128, 256], O)
        self.clast = self.dram("clast", [DEPTH, 30, 1024], O)
        self.kso = self.dram("kso", [DEPTH, NSEQ_S, 128, 256], O)
        self.vso = self.dram("vso", [DEPTH, NSEQ_S, 128, 256], O)
        self.cso = self.dram("cso", [DEPTH, NSEQ_S, 30, 1024], O)

        sb = self.sb
        T = 544
        self.xT = sb("xT", [128, 16, T], F32)
        self.hT = sb("hT", [128, 16, T], BF16)
        self.qa = sb("qa", [128, 8, T], BF16)
        self.oT = sb("oT", [128, 8, T], BF16)
        self.kbl = sb("kbl", [128, 4, 128 + T], BF16)
        self.kbh = sb("kbh", [128, 4, 128 + T], BF16)
        self.vb = sb("vb", [128, 2, 128 + T], BF16)
        self.kf = sb("kf", [128, 4, 160], F32)
        self.vf = sb("vf", [128, 2, 160], F32)
        self.u = sb("u", [128, 8, T], F32)
        self.mg = self.u[:].rearrange("p c t -> p (c t)").bitcast(BF16).rearrange("p (c t) -> p c t", c=16)
        self.us = sb("us", [128, 8, NSEQ_S, 38], F32)
        self.yc = sb("yc", [128, 8, T], F32)
        ycf = self.yc[:].rearrange("p c t -> p (c t)")
        self.fa = ycf.bitcast(BF16).rearrange("p (c t) -> p c t", c=16)
        self.fin = ycf[:, 0:2048].rearrange("p (c t) -> p c t", c=16)
        self.rstd = sb("rstd", [128, T], F32)
        self.tmps = [sb("tmp%d" % i, [128, 512], F32) for i in range(NTMP)]
        self.tbs = [sb("tb%d" % i, [128, 512], BF16) for i in range(1)]
        self.wsl = [sb("wsl%d" % i, [128, 2048], BF16) for i in range(NWSLOT)]
        self.vtl = [sb("vtl%d" % i, [128, 4, 128], BF16) for i in range(3)]
        self.vth = [sb("vth%d" % i, [128, 4, 128], BF16) for i in range(3)]
        self.kcl = sb("kcl", [128, 4, 128], BF16)
        self.kch = sb("kch", [128, 4, 128], BF16)
        self.vcl = sb("vcl", [128, 4, 128], BF16)
        self.vch = sb("vch", [128, 4, 128], BF16)
        self.vol = sb("vol", [128, 4, 128], BF16)
        self.voh = sb("voh", [128, 4, 128], BF16)
        self.pn = [sb("pn%d" % i, [128, 2, 256], BF16) for i in range(1)]
        self.ptsb = [sb("ptsb%d" % i, [128, 4, 128], BF16) for i in range(1)]
        self.st = [sb("st%d" % i, [128, 16], F32) for i in range(2)]
        self.masks = sb("masks", [128, 3, 256], F32)
        self.Ct = sb("Ct", [128, T], F32)
        self.St = sb("St", [128, T], F32)
        self.valid = sb("valid", [128, 512], F32)
        self.pvec = sb("pvecs", [128, NV], F32)
        self.identf = sb("identfs", [128, 128], F32)
        self.identb = sb("identb", [128, 128], BF16)
        self.rmf = sb("rmf", [128, 128], F32)
        self.rmb = sb("rmb", [128, 128], BF16)
        self.ones_r = sb("ones_r", [128, 128], F32)
        self.ones_l = sb("ones_l", [128, 128], F32)
        self.cK = [sb("cK%d" % l, [128, 4, 128], BF16) for l in range(dp)]
        self.cV = [sb("cV%d" % l, [128, 2, 128], BF16) for l in range(dp)]
        self.cU = [sb("cU%d" % l, [128, 8, 30], F32) for l in range(dp)]
        self.xs = sb("xs", [128, D], F32)
        self.pjb = [self.ps("pj%d" % i, [128, 512], F32) for i in range(6)]
        self.pS = self.ps("pS", [128, 2, 256], F32)
        pto = self.ps("pTO", [128, 512], F32)
        self.pO = pto[:, 0:256]
        self.pT = pto[:, 256:512].bitcast(BF16).rearrange("p (j t) -> p j t", j=4)

    def _consts(self):
        S = self.S
        toks = []
        toks.append(self.dma_in(self.pvec[:], self.pvec_d, ["pvec"], "c0"))
        toks.append(self.dma_in(self.masks[:], self.masks_d.rearrange("m p k -> p m k"), ["masks"], "c1"))
        toks.append(self.dma_in(self.valid[:], self.valid_d, ["valid"], "c2"))
        toks.append(self.dma_in(self.identf[:], self.ident_d, ["identf"], "c3"))
        toks.append(self.dma_in(self.rmf[:], self.rm_d, ["rmf"], "c4"))
        self.dve_copy(self.identb[:], self.identf[:], ["identf"], ["identb"])
        self.dve_copy(self.rmb[:], self.rmf[:], ["rmf"], ["rmb"])
        self.DVE(lambda e: e.memset(self.ones_r[:], 1.0 / 2048.0), (), ["ones_r"])
        self.DVE(lambda e: e.memset(self.ones_l[:], 1.0 / 1024.0), (), ["ones_l"])
        for i in range(3):
            self.DVE(lambda e, i=i: e.memset(self.vtl[i][:], 0.0), (), [("vt", i)])
            self.DVE(lambda e, i=i: e.memset(self.vth[i][:], 0.0), (), [("vt", i)])
        for t_, k in ((self.vcl, "vc"), (self.vch, "vc"), (self.vol, "vo"), (self.voh, "vo")):
            self.DVE(lambda e, t_=t_: e.memset(t_[:], 0.0), (), [k])
        self.DVE(lambda e: e.memset(self.u[:], 0.0), (), [("u", c) for c in range(8)])
        self.DVE(lambda e: e.memset(self.kbl[:], 0.0), (), [("kb", g) for g in range(4)])
        self.DVE(lambda e: e.memset(self.kbh[:], 0.0), (), [("kb", g) for g in range(4)])
        self.DVE(lambda e: e.memset(self.kcl[:], 0.0), (), ["kc"])
        self.DVE(lambda e: e.memset(self.kch[:], 0.0), (), ["kc"])
        self.DVE(lambda e: e.memset(self.vb[:], 0.0), (), [("vb", c) for c in range(2)])
        last = self.DVE(lambda e: e.memset(self.rstd[:], 0.0), (), ["rstd"])
        for eng in ("pe", "act", "dve", "pool", "sp"):
            S.op(eng, None, extra=toks + [last], signal=False)
            S.pending[eng] = []

    def pieces(self, g, c0):
        out = []
        if c0 < 512:
            out.append((c0, 512))
        if g == 0:
            out.append((512, 544))
        return out

    def group(self, g):
        self.dma_in(self.Ct[:], self.rope_d[g, 0], ["Ct"], "rope")
        self.dma_in(self.St[:], self.rope_d[g, 1], ["St"], "rope2")
        loads = [(4 * g + s, 128, 128 * s) for s in range(4)]
        if g == 0:
            loads.append((12, 32, 512))
        for (slot, n, c0) in loads:
            self.dma_in(self.xs[0:n, :], self.xin[slot * 128:slot * 128 + n, :], ["xs"], "xin")
            for c4 in range(4):
                bank, bk = self.pj()
                bv = bank[:].rearrange("p (j t) -> p j t", j=4)
                for j in range(4):
                    c = c4 * 4 + j
                    self.tr(bv[:, j, 0:n], self.xs[0:n, c * 128:(c + 1) * 128], self.identf[0:n, 0:n], ["xs"], [bk], j == 3)
                self.act_copy(self.xT[:, c4 * 4:(c4 + 1) * 4, c0:c0 + n], bv[:, :, 0:n], [bk], [("x", c) for c in range(c4 * 4, c4 * 4 + 4)])
        if STOP == "load":
            return
        for l in range(self.depth):
            self.layer(g, l)
        if STOP == "nofinal":
            return
        self.final(g)

    def rmsnorm(self, pcs, gcol, out_fn, okeys):
        for (a, b) in pcs:
            n = b - a
            bank, bk = self.pj()
            for c in range(NCH):
                t, tk = self.tmp()
                self.act_copy(t[:, 0:n], self.xT[:, c, a:b], [("x", c)], [tk], func=AF.Square)
                self.mm(bank[:, 0:n], self.ones_r[:], t[:, 0:n], c == 0, c == NCH - 1, [tk], [bk], sig=True)
            self.rsqrt_inplace(self.rstd[:, a:b], "rstd", bank[:, 0:n], bk)
            for c in range(NCH):
                self.stt(out_fn(c, a, b), self.xT[:, c, a:b], self.pvec[:, gcol + c:gcol + c + 1], self.rstd[:, a:b],
                         ALU.mult, ALU.mult, [("x", c), "rstd"], [okeys(c)])

    def proj(self, wv, wk, nk, rhs_fn, rkeys, a, b):
        n = b - a
        bank, bk = self.pj()
        for k in range(nk):
            self.mm(bank[:, 0:n], wv[:, k, :], rhs_fn(k, a, b), k == 0, k == nk - 1, [wk, rkeys(k)], [bk])
        return bank, bk

    def layer(self, g, l):
        self.bctr = l * NBLK_L
        self.cur_g = g
        pb = l * PV_L
        c_ext = 128 * l if g == 0 else 0
        c_full = 128 * (l + 1) if g == 0 else 0
        p_ext = self.pieces(g, c_ext)
        p_full = self.pieces(g, c_full)
        hrhs = lambda k, a, b: self.hT[:, k, a:b]
        hkey = lambda k: ("h", k)
        if g > 0:
            self.act_copy(self.kbl[0:64, :, 0:128], self.cK[l][0:64], [("cK", l)], [("kb", i) for i in range(4)])
            self.act_copy(self.kbh[64:128, :, 0:128], self.cK[l][64:128], [("cK", l)], [("kb", i) for i in range(4)])
            self.act_copy(self.vb[:, :, 0:128], self.cV[l][:], [("cV", l)], [("vb", i) for i in range(2)])
            self.dve_copy(self.u[:, :, 0:30], self.cU[l][:], [("cU", l)], [("u", c) for c in range(8)])
        else:
            for j in range(NSEQ_S):
                self.dma_in(self.xs[0:30, 0:1024], self.sc[l, j], ["xs"], "cst")
                for hf in range(2):
                    bank, bk = self.pj()
                    bv = bank[:].rearrange("p (j t) -> p j t", j=4)
                    for cc in range(4):
                        c = hf * 4 + cc
                        self.tr(bv[:, cc, 0:30], self.xs[0:30, c * 128:(c + 1) * 128], self.identf[0:30, 0:30], ["xs"], [bk], cc == 3)
                    self.act_copy(self.us[:, hf * 4:(hf + 1) * 4, j, 0:30], bv[:, :, 0:30], [bk], ["us"])
        if STOP == "carry":
            return
        self.rmsnorm(p_ext, pb + 0, lambda c, a, b: self.hT[:, c, a:b], hkey)
        if STOP == "norm1":
            return
        for gi in range(4):
            wv, wk = self.load_w(self.win[l, 8 + gi], 16, 128)
            if DBG == "nomm":
                self.act_copy(self.tbs[0][:, 0:128], wv[:, 0, :], [wk], [("tb", 0)])
                continue
            for (a, b) in p_ext:
                if DBG == "p512" and a >= 512:
                    continue
                if DBG == "p32" and a < 512:
                    continue
                bank, bk = self.proj(wv, wk, 16, hrhs, hkey, a, b)
                dests = [(self.kbl[0:64, gi, 128 + a:128 + b], ("kb", gi), 0, (0, 64)),
                         (self.kbh[64:128, gi, 128 + a:128 + b], ("kb", gi), 0, (64, 128))]
                if b > 384:
                    lo = max(a, 384)
                    dests.append((self.kf[:, gi, lo - 384:b - 384], "kf", lo - a))
                if DBG in ("norot", "p512", "p32", "dvesecond"):
                    self.act_copy(dests[0][0], bank[0:64, 0:b - a], [bk], [dests[0][1]])
                elif DBG == "noproj":
                    pass
                else:
                    self.rot_evac(bank, bk, a, b, dests)
        if STOP == "kproj":
            return
        for c in range(2):
            wv, wk = self.load_w(self.win[l, 12 + c], 16, 128)
            for (a, b) in p_ext:
                n = b - a
                bank, bk = self.proj(wv, wk, 16, hrhs, hkey, a, b)
                self.act_copy(self.vb[:, c, 128 + a:128 + b], bank[:, 0:n], [bk], [("vb", c)])
                if b > 384:
                    lo = max(a, 384)
                    self.dve_copy(self.vf[:, c, lo - 384:b - 384], bank[:, lo - a:n], [bk], ["vf"])
        if STOP == "vproj":
            return
        for c in range(8):
            wv1, wk1 = self.load_w(self.win[l, 14 + 2 * c], 16, 128)
            wv2, wk2 = self.load_w(self.win[l, 15 + 2 * c], 16, 128)
            for (a, b) in p_ext:
                n = b - a
                b1, k1 = self.proj(wv1, wk1, 16, hrhs, hkey, a, b)
                b2, k2 = self.proj(wv2, wk2, 16, hrhs, hkey, a, b)
                t, tk = self.tmp()
                self.act_copy(t[:, 0:n], b2[:, 0:n], [k2], [tk], func=AF.Sigmoid)
                if a < 512:
                    self.tt(self.u[:, c, 30 + a:30 + b], b1[:, 0:n], t[:, 0:n], ALU.mult, [k1, tk], [("u", c)])
                else:
                    self.tt(self.us[:, c, :, 30:38], b1[:, 0:32].rearrange("p (s t) -> p s t", s=NSEQ_S),
                            t[:, 0:32].rearrange("p (s t) -> p s t", s=NSEQ_S), ALU.mult, [k1, tk], ["us"])
        if g == 0:
            ukeys = [("u", c) for c in range(8)]
            if c_ext < 512:
                nv_ = 512 - c_ext
                self.tt(self.u[:, :, 30 + c_ext:542], self.u[:, :, 30 + c_ext:542],
                        self.valid[:, c_ext:512].unsqueeze(1).to_broadcast([128, 8, nv_]), ALU.mult, ukeys, ukeys)
        if STOP == "glu":
            return
        for c in range(8):
            wv, wk = self.load_w(self.win[l, c], 16, 128)
            for (a, b) in p_full:
                bank, bk = self.proj(wv, wk, 16, hrhs, hkey, a, b)
                self.rot_evac(bank, bk, a, b, [(self.qa[:, c, a:b], ("qa", c))])
        if STOP == "qproj":
            return
        if g < self.ngroups - 1:
            self.act_copy(self.cK[l][0:64], self.kbl[0:64, :, 512:640], [("kb", i) for i in range(4)], [("cK", l)])
            self.act_copy(self.cK[l][64:128], self.kbh[64:128, :, 512:640], [("kb", i) for i in range(4)], [("cK", l)])
            self.act_copy(self.cV[l][:], self.vb[:, :, 512:640], [("vb", i) for i in range(2)], [("cV", l)])
            self.dve_copy(self.cU[l][:], self.u[:, :, 512:542], [("u", c) for c in range(8)], [("cU", l)])
        if g == self.ngroups - 1:
            self.out_rows(l, 128, 0, 482, self.klast[l], self.vlast[l], self.clast[l])
        if g == 0:
            self.out_rows(l, 32, 128, None, None, None, None)
        if STOP == "inproj":
            return
        self.deferred = self.conv_taps(g, l, p_full)
        n_units = 8 * max(0, 4 - c_full // 128) + (8 * NSEQ_S if g == 0 else 0)
        self.per_unit = -(-len(self.deferred) // max(1, n_units))
        self.attention(g, l, c_full)
        if STOP == "attn":
            return
        self.conv(g, l, p_full)
        for c in range(NCH):
            wa, ka = self.load_w(self.wao[l, c], 8, 128)
            wc, kc_ = self.load_w(self.wco[l, c], 8, 128)
            wga, kga = self.load_w(self.win[l, 30 + 2 * c], 16, 128)
            wgb, kgb = self.load_w(self.win[l, 31 + 2 * c], 16, 128)
            for (a, b) in p_full:
                n = b - a
                pa, pak = self.proj(wa, ka, 8, lambda k, a, b: self.oT[:, k, a:b], lambda k: ("o", k), a, b)
                pga, pgak = self.proj(wga, kga, 16, hrhs, hkey, a, b)
                t1, t1k = self.tmp()
                self.act_copy(t1[:, 0:n], pga[:, 0:n], [pgak], [t1k], func=AF.Sigmoid)
                self.tt(t1[:, 0:n], pa[:, 0:n], t1[:, 0:n], ALU.mult, [pak, t1k], [t1k])
                pc, pck = self.proj(wc, kc_, 8, lambda k, a, b: self.qa[:, k, a:b], lambda k: ("qa", k), a, b)
                pgb, pgbk = self.proj(wgb, kgb, 16, hrhs, hkey, a, b)
                t2, t2k = self.tmp()
                self.act_copy(t2[:, 0:n], pgb[:, 0:n], [pgbk], [t2k], func=AF.Sigmoid)
                self.stt(t2[:, 0:n], pc[:, 0:n], self.pvec[:, pb + 32 + c:pb + 33 + c], t2[:, 0:n], ALU.add, ALU.mult,
                         [pck, t2k], [t2k])
                self.tt(self.mg[:, c, a:b], t1[:, 0:n], t2[:, 0:n], ALU.add, [t1k, t2k], [("u", c // 2)])
        if STOP == "merge":
            return
        for c in range(NCH):
            wv, wk = self.load_w(self.wo[l, c], 16, 128)
            for (a, b) in p_full:
                n = b - a
                bank, bk = self.proj(wv, wk, 16, lambda k, a, b: self.mg[:, k, a:b], lambda k: ("u", k // 2), a, b)
                self.tt(self.xT[:, c, a:b], self.xT[:, c, a:b], bank[:, 0:n], ALU.add, [bk, ("x", c)], [("x", c)])
        self.rmsnorm(p_full, pb + 16, lambda c, a, b: self.hT[:, c, a:b], hkey)
        for hg in range(4):
            for jj in range(11):
                j = hg * 11 + jj
                w1, k1 = self.load_w(self.wf1[l, 2 * j], 16, 128)
                w2, k2 = self.load_w(self.wf1[l, 2 * j + 1], 16, 128)
                for (a, b) in p_full:
                    n = b - a
                    b1, bk1 = self.proj(w1, k1, 16, hrhs, hkey, a, b)
                    b2, bk2 = self.proj(w2, k2, 16, hrhs, hkey, a, b)
                    t, tk = self.tmp()
                    self.act_copy(t[:, 0:n], b1[:, 0:n], [bk1], [tk], func=AF.Silu)
                    self.tt(self.fa[:, jj, a:b], t[:, 0:n], b2[:, 0:n], ALU.mult, [tk, bk2], [("yc", jj // 2)])
            for c in range(NCH):
                wv, wk = self.load_w(self.wf2[l, hg, c], 11, 128)
                for (a, b) in p_full:
                    n = b - a
                    bank, bk = self.proj(wv, wk, 11, lambda k, a, b: self.fa[:, k, a:b], lambda k: ("yc", k // 2), a, b)
                    self.tt(self.xT[:, c, a:b], self.xT[:, c, a:b], bank[:, 0:n], ALU.add, [bk, ("x", c)], [("x", c)])

    def rot_evac(self, bank, bk, a, b, dests):
        n = b - a
        i = self.nxt("tb", 1)
        tb, tbk = self.tbs[i], ("tb", i)
        self.act_copy(tb[:, 0:n], bank[:, 0:n], [bk], [tbk])
        t1, t1k = self.tmp()
        self.tt(t1[:, 0:n], bank[:, 0:n], self.Ct[:, a:b], ALU.mult, [bk, "Ct"] + ([tbk] if DBG == "rot1s" else []), [t1k])
        if DBG in ("rot1", "rot1s"):
            for d in dests:
                off = d[2] if len(d) > 2 else 0
                self.act_copy(d[0], t1[:, off:n], [t1k], [d[1]])
            return
        b2, b2k = self.pj()
        self.mm(b2[:, 0:n], self.rmb[:], tb[:, 0:n], True, True, [tbk], [b2k])
        t2, t2k = self.tmp()
        self.tt(t2[:, 0:n], b2[:, 0:n], self.St[:, a:b], ALU.mult, [b2k, "St"], [t2k])
        if DBG == "rot2":
            for d in dests:
                off = d[2] if len(d) > 2 else 0
                self.act_copy(d[0], t2[:, off:n], [t1k, t2k], [d[1]])
            return
        for d in dests:
            off = d[2] if len(d) > 2 else 0
            p0, p1 = d[3] if len(d) > 3 else (0, 128)
            self.tt(d[0], t1[p0:p1, off:n], t2[p0:p1, off:n], ALU.add, [t1k, t2k], [d[1]])

    def out_rows(self, l, n, kcol, ucol, kdst, vdst, cdst):
        bank, bk = self.pj()
        bv = bank[:].rearrange("p (j t) -> p j t", j=4)
        for gi in range(4):
            self.tr(bv[0:n, gi, :], self.kf[:, gi, kcol:kcol + n], self.identf[:], ["kf"], [bk], gi == 3)
        self.act_copy(self.xs[0:n, 0:256].rearrange("p (g d) -> p g d", g=4), bv[0:n, :, 0:64], [bk], ["xs"])
        bank, bk = self.pj()
        bv = bank[:].rearrange("p (j t) -> p j t", j=4)
        for c in range(2):
            self.tr(bv[0:n, c, :], self.vf[:, c, kcol:kcol + n], self.identf[:], ["vf"], [bk], c == 1)
        self.act_copy(self.xs[0:n, 256:512].rearrange("p (g d) -> p g d", g=2), bv[0:n, 0:2, :], [bk], ["xs"])
        m = 30 if n == 128 else 32
        for hf in range(2):
            if n != 128:
                tq, tqk = self.tmp()
                for cc in range(4):
                    self.dve_copy(tq[:, cc * 32:(cc + 1) * 32].rearrange("p (s t) -> p s t", s=NSEQ_S),
                                  self.us[:, hf * 4 + cc, :, 30:38], ["us"], [tqk])
            bank, bk = self.pj()
            bv = bank[:].rearrange("p (j t) -> p j t", j=4)
            for cc in range(4):
                c = hf * 4 + cc
                src = self.u[:, c, 30 + ucol:30 + ucol + 30] if n == 128 else tq[:, cc * 32:(cc + 1) * 32]
                rk = ("u", c) if n == 128 else tqk
                self.tr(bv[0:m, cc, :], src, self.identf[:], [rk], [bk], cc == 3)
            self.act_copy(self.xs[0:m, 1024 + hf * 512:1536 + hf * 512].rearrange("p (g d) -> p g d", g=4), bv[0:m, :, :], [bk], ["xs"])
        if n == 128:
            self.dma_out(kdst, self.xs[0:128, 0:256], ["xs"])
            self.dma_out(vdst, self.xs[0:128, 256:512], ["xs"])
            self.dma_out(cdst, self.xs[0:30, 1024:2048], ["xs"])
        else:
            self.dma_out(self.kso[l, :, 0:120, :], self.ck[l, :, 8:128, :], [])
            self.dma_out(self.vso[l, :, 0:120, :], self.cv[l, :, 8:128, :], [])
            self.dma_out(self.cso[l, :, 0:22, :], self.sc[l, :, 8:30, :], [])
            for j in range(NSEQ_S):
                self.dma_out(self.kso[l, j, 120:128, :], self.xs[8 * j:8 * j + 8, 0:256], ["xs"])
                self.dma_out(self.vso[l, j, 120:128, :], self.xs[8 * j:8 * j + 8, 256:512], ["xs"])
                self.dma_out(self.cso[l, j, 22:30, :], self.xs[8 * j:8 * j + 8, 1024:2048], ["xs"])

    def build_vt(self, i, col0):
        for c in range(2):
            self.tr(self.pT[:, c, :], self.vb[:, c, col0:col0 + 128], self.identb[:], [("vb", c)], ["pTO"], c == 1)
        src = self.pT[:, 0:2, :].rearrange("p c (h d) -> p (c h) d", h=2)
        self.act_copy(self.vtl[i][:, :, 0:64], src, ["pTO"], [("vt", i)])
        self.dve_copy(self.vth[i][:, :, 64:128], src, ["pTO"], [("vt", i)])

    def attention(self, g, l, c_full):
        s0 = c_full // 128
        prev = None
        for s in range(s0, 4):
            if prev is None:
                prev = self.nxt("vt", 3)
                self.build_vt(prev, 128 * s)
            own = self.nxt("vt", 3)
            self.build_vt(own, 128 + 128 * s)
            for un in range(8):
                if DBG == "vtonly":
                    break
                kk4 = [("kb", i) for i in range(4)]
                segs = [((self.kbl[:, :, 128 * s:128 * s + 128], self.kbh[:, :, 128 * s:128 * s + 128]), kk4, self.vtl[prev], self.vth[prev], ("vt", prev), 128),
                        ((self.kbl[:, :, 128 + 128 * s:256 + 128 * s], self.kbh[:, :, 128 + 128 * s:256 + 128 * s]), kk4, self.vtl[own], self.vth[own], ("vt", own), 128)]
                mi = 0
                if g == 0 and s == 3:
                    mi = 1
                if g == 1 and s == 0:
                    mi = 2
                self.unit(l, un, 128, 128 * s, segs, mi)
            prev = own
        if g == 0 and DBG not in ("vtonly", "nosample", "qk", "exp", "pn", "ptr"):
            for j in range(NSEQ_S):
                stg, sk_ = self.tmp()
                self.dma_in(stg[:, 0:256], self.ck[l, j], [sk_], "stg")
                src = stg[:, 0:256].rearrange("p (g d) -> p g d", g=4)
                stgd_, sdk = self.tmp()
                stgd = stgd_[:].rearrange("p (g d) -> p g d", g=4)
                self.act_copy(stgd[:, :, 0:64], src, [sk_], [sdk])
                self.dve_copy(stgd[:, :, 64:128], src, [sk_], [sdk])
                bank, bk = self.pj()
                bv = bank[:].rearrange("p (j t) -> p j t", j=4)
                for gi in range(4):
                    self.tr(bv[:, gi, :], stgd[:, gi, :], self.identf[:], [sdk], [bk], gi == 3)
                self.act_copy(self.kcl[0:64], bv[0:64], [bk], ["kc"])
                self.act_copy(self.kch[64:128], bv[64:128], [bk], ["kc"])
                stg, sk_ = self.tmp()
                self.dma_in(stg[:, 0:256], self.cv[l, j], [sk_], "stg")
                src = stg[:, 0:256].rearrange("p (g d) -> p g d", g=4)
                self.act_copy(self.vcl[:, :, 0:64], src, [sk_], ["vc"])
                self.dve_copy(self.vch[:, :, 64:128], src, [sk_], ["vc"])
                col = 128 + 512 + TS * j
                for c in range(2):
                    self.tr(self.pT[0:TS, c, :], self.vb[:, c, col:col + TS], self.identb[:], [("vb", c)], ["pTO"], c == 1)
                src = self.pT[0:TS, 0:2, :].rearrange("p c (h d) -> p (c h) d", h=2)
                self.act_copy(self.vol[0:TS, :, 0:64], src, ["pTO"], ["vo"])
                self.dve_copy(self.voh[0:TS, :, 64:128], src, ["pTO"], ["vo"])
                for un in range(8):
                    segs = [((self.kcl[:], self.kch[:]), ["kc"], self.vcl, self.vch, "vc", 128),
                            ((self.kbl[:, :, col:col + TS], self.kbh[:, :, col:col + TS]), [("kb", i) for i in range(4)], self.vol, self.voh, "vo", TS)]
                    self.unit(l, un, TS, 512 + TS * j, segs, 0)

    def unit(self, l, un, nq, qc, segs, mi):
        gk = un // 2
        nkt = sum(sg[5] for sg in segs)
        pS = self.pS
        for h in range(2):
            po = 64 * h
            if DBG == "qkh0":
                po = 0
            off = 0
            for si, (kap, kkeys, vl, vh, vk, nk) in enumerate(segs):
                self.mm(pS[0:nq, h, off:off + nk], self.qa[:, un, qc:qc + nq], kap[h][:, gk, 0:nk], True, True,
                        [("qa", un), kkeys[gk % len(kkeys)]], ["pS"], sig=(h == 1 and si == len(segs) - 1))
                off += nk
        if DBG in ("qkonly", "qkh0"):
            self.act_copy(self.oT[0:nq, un, qc:qc + nq], pS[0:nq, 0, 0:nq], ["pS"], [("o", un)])
            return
        t, tk = self.tmp()
        sm = t[:].rearrange("p (h k) -> p h k", h=2)
        si_ = self.nxt("st", 2)
        st, stk = self.st[si_], ("st", si_)
        self.tt(sm[0:nq, :, 0:nkt], pS[0:nq, :, 0:nkt], self.masks[0:nq, mi:mi + 1, 0:nkt].to_broadcast([nq, 2, nkt]), ALU.add,
                ["pS"], [tk])
        self.DVE(lambda e: e.tensor_reduce(out=st[0:nq, 0:2], in_=sm[0:nq, :, 0:nkt], axis=AX.X, op=ALU.max), [tk], [stk])
        self.ts(st[0:nq, 2:4], st[0:nq, 0:2], -100.0, -SCALE, ALU.max, ALU.mult, [stk], [stk])
        self.DVE(lambda e: e.memset(st[0:nq, 4:6], 0.0), [stk], [stk])
        if DBG == "qk":
            return
        for h in range(2):
            self.ACT(lambda e, h=h: e.activation(out=sm[0:nq, h, 0:nkt], in_=sm[0:nq, h, 0:nkt], func=AF.Exp, bias=st[0:nq, 2 + h:3 + h],
                                                  scale=SCALE, accum_out=st[0:nq, 4 + h:5 + h]), [tk, stk], [tk, stk])
        sk = PV_SINK + l * 16 + 2 * un
        self.tt(st[0:nq, 6:8], st[0:nq, 2:4], self.pvec[0:nq, sk:sk + 2], ALU.add, [stk], [stk])
        self.act_copy(st[0:nq, 6:8], st[0:nq, 6:8], [stk], [stk], func=AF.Exp)
        self.tt(st[0:nq, 8:10], st[0:nq, 4:6], st[0:nq, 6:8], ALU.add, [stk], [stk])
        self.DVE(lambda e: e.reciprocal(out=st[0:nq, 10:12], in_=st[0:nq, 8:10]), [stk], [stk])
        if DBG == "exp":
            return
        pi = self.nxt("pn", 1)
        pn, pnk = self.pn[pi], ("pn", pi)
        self.tt(pn[0:nq, :, 0:nkt], sm[0:nq, :, 0:nkt], st[0:nq, 10:12].unsqueeze(2).to_broadcast([nq, 2, nkt]), ALU.mult,
                [tk, stk], [pnk])
        if DBG == "pn":
            return
        cnt_ = 0
        for h in range(2):
            off = 0
            for si, sg in enumerate(segs):
                nk = sg[5]
                cnt_ += 1
                self.tr(self.pT[0:nk, si * 2 + h, 0:nq], pn[0:nq, h, off:off + nk], self.identb[0:nq, 0:nq], [pnk], ["pTO"],
                        cnt_ == 2 * len(segs))
                off += nk
        qi = self.nxt("ptsb", 1)
        ptsb, ptk = self.ptsb[qi], ("ptsb", qi)
        nks = [sg[5] for sg in segs]
        if len(set(nks)) == 1:
            self.act_copy(ptsb[0:nks[0], :, 0:nq], self.pT[0:nks[0], :, 0:nq], ["pTO"], [ptk])
        else:
            for si, nk in enumerate(nks):
                self.act_copy(ptsb[0:nk, 2 * si:2 * si + 2, 0:nq], self.pT[0:nk, 2 * si:2 * si + 2, 0:nq], ["pTO"], [ptk])
        if DBG == "ptr":
            return
        idx = 0
        last = 2 * len(segs) - 1
        for h in range(2):
            for si, (kap, kkeys, vl, vh, vk, nk) in enumerate(segs):
                vsel = vl if h == 0 else vh
                self.mm(self.pO[:, 0:nq], vsel[0:nk, gk, :], ptsb[0:nk, si * 2 + h, 0:nq], idx == 0, idx == last, [vk, ptk], ["pTO"])
                idx += 1
        self.act_copy(self.oT[:, un, qc:qc + nq], self.pO[:, 0:nq], ["pTO"], [("o", un)])
        self.drain(self.per_unit)

    def conv_taps(self, g, l, p_full):
        pb = l * PV_L
        wcol = pb + 72
        ops = []
        for c in range(8):
            for (a, b) in p_full:
                if a < 512:
                    acc = self.yc[:, c, a:b]
                    srcs = [self.u[:, c, a + j:b + j] for j in range(31)]
                    rk = ("u", c)
                else:
                    acc = self.yc[:, c, 512:544].rearrange("p (s t) -> p s t", s=NSEQ_S)
                    srcs = [self.us[:, c, :, j:j + TS] for j in range(31)]
                    rk = "us"
                for j in range(31):
                    wj = self.pvec[:, wcol + c * 31 + j:wcol + c * 31 + j + 1]
                    if j == 0:
                        ops.append(lambda acc=acc, s0=srcs[0], wj=wj, c=c, rk=rk: self.ts(
                            acc, s0, wj, self.pvec[:, pb + 48 + c:pb + 49 + c], ALU.mult, ALU.add, [rk], [("yc", c)]))
                    else:
                        ops.append(lambda acc=acc, sj=srcs[j], wj=wj, c=c, rk=rk: self.stt(
                            acc, sj, wj, acc, ALU.mult, ALU.add, [rk, ("yc", c)], [("yc", c)]))
        nt = 31
        groups = [ops[i:i + nt] for i in range(0, len(ops), nt)]
        return [grp[j] for j in range(nt) for grp in groups]

    def drain(self, n=None):
        q = self.deferred
        k = len(q) if n is None else min(n, len(q))
        for _ in range(k):
            q.pop(0)()

    def conv(self, g, l, p_full):
        pb = l * PV_L
        self.drain()
        for (a, b) in p_full:
            n = b - a
            bmu, bmk = self.pj()
            for c in range(8):
                self.mm(bmu[:, 0:n], self.ones_l[:], self.yc[:, c, a:b], c == 0, c == 7, [("yc", c)], [bmk])
            bvar, bvk = self.pj()
            for c in range(8):
                self.tt(self.yc[:, c, a:b], self.yc[:, c, a:b], bmu[:, 0:n], ALU.subtract, [("yc", c), bmk], [("yc", c)])
                t, tk = self.tmp()
                self.act_copy(t[:, 0:n], self.yc[:, c, a:b], [("yc", c)], [tk], func=AF.Square)
                self.mm(bvar[:, 0:n], self.ones_l[:], t[:, 0:n], c == 0, c == 7, [tk], [bvk], sig=True)
            self.rsqrt_inplace(self.rstd[:, a:b], "rstd", bvar[:, 0:n], bvk)
            for c in range(8):
                t, tk = self.tmp()
                self.tt(t[:, 0:n], self.yc[:, c, a:b], self.rstd[:, a:b], ALU.mult, [("yc", c), "rstd"], [tk])
                self.act_copy(self.qa[:, c, a:b], t[:, 0:n], [tk], [("qa", c)], func=AF.Silu,
                              scale=self.pvec[:, pb + 56 + c:pb + 57 + c], bias=self.pvec[:, pb + 64 + c:pb + 65 + c])

    def final(self, g):
        if g == 0:
            pcs, outs = [(512, 544)], [(512, 32, 1024)]
        else:
            pcs, outs = [(0, 512)], [(128 * s, 128, (g - 1) * 512 + 128 * s) for s in range(4)]
        fkeys = [("yc", i) for i in range(4)]
        for (a, b) in pcs:
            n = b - a
            bank, bk = self.pj()
            for c in range(NCH):
                t, tk = self.tmp()
                self.act_copy(t[:, 0:n], self.xT[:, c, a:b], [("x", c)], [tk], func=AF.Square)
                self.mm(bank[:, 0:n], self.ones_r[:], t[:, 0:n], c == 0, c == NCH - 1, [tk], [bk], sig=True)
            self.rsqrt_inplace(self.rstd[:, a:b], "rstd", bank[:, 0:n], bk)
        for (c0, n, row0) in outs:
            for c in range(NCH):
                self.stt(self.fin[:, c, 0:n], self.xT[:, c, c0:c0 + n], self.pvec[:, PV_FIN + c:PV_FIN + c + 1], self.rstd[:, c0:c0 + n],
                         ALU.mult, ALU.mult, [("x", c), "rstd"] + fkeys, fkeys)
            for c4 in range(4):
                bank, bk = self.pj()
                bv = bank[:].rearrange("p (j t) -> p j t", j=4)
                for j in range(4):
                    self.tr(bv[0:n, j, :], self.fin[:, c4 * 4 + j, 0:n], self.identf[:], fkeys, [bk], j == 3)
                self.act_copy(self.xs[0:n, c4 * 512:(c4 + 1) * 512], bank[0:n, :], [bk], ["xs"])
            self.dma_out(self.y[row0:row0 + n, :], self.xs[0:n, :], ["xs"])


_W_IN_ORDER = None


def _win_cols():
    chunks = []
    for c in range(8):
        chunks.append(np.arange(128 * c, 128 * c + 128))
    for g in range(4):
        cols = np.arange(1024 + 64 * g, 1024 + 64 * g + 64)
        chunks.append(np.concatenate([cols, cols]))
    for c in range(2):
        chunks.append(np.arange(1280 + 128 * c, 1280 + 128 * c + 128))
    for c in range(8):
        chunks.append(np.arange(1536 + 128 * c, 1536 + 128 * c + 128))
        chunks.append(np.arange(2560 + 128 * c, 2560 + 128 * c + 128))
    for c in range(16):
        chunks.append(np.arange(3584 + 128 * c, 3584 + 128 * c + 128))
        chunks.append(np.arange(5632 + 128 * c, 5632 + 128 * c + 128))
    return np.concatenate(chunks)


def _blockify(w, ncolblk=128):
    L, K, N = w.shape
    nk = K // 128
    nb = N // ncolblk
    r = w.reshape(L, nk, 128, nb, ncolblk).transpose(0, 3, 2, 1, 4)
    return np.ascontiguousarray(r).reshape(L, nb, 128, nk * ncolblk)


def _rope_tables(pos):
    half = 8
    inv = (np.float32(500000.0) ** (-np.arange(half, dtype=np.float32) * np.float32(2.0) / np.float32(16.0))).astype(np.float32)
    ang = pos.astype(np.float32)[None, :] * inv[:, None]
    cos = np.cos(ang).astype(np.float32)
    sin = np.sin(ang).astype(np.float32)
    T = pos.shape[0]
    C = np.ones((64, T), np.float32)
    Sn = np.zeros((64, T), np.float32)
    C[0:8] = cos
    C[8:16] = cos
    Sn[0:8] = sin
    Sn[8:16] = sin
    return np.concatenate([C, C], 0), np.concatenate([Sn, Sn], 0)


_PROG_CACHE = {}


RUN_DEPTH = DEPTH
RUN_CORES = list(range(8))
STOP = None
DBG = None


def _get_prog():
    if "nc" not in _PROG_CACHE:
        _PROG_CACHE["nc"] = Prog(depth=RUN_DEPTH).build()
    return _PROG_CACHE["nc"]


def kernel(x_prompt, x_sample, cache_k, cache_v, state_conv, meta_tokens, w_in, attn_sinks, w_attn_out,
           conv_dw, conv_dw_bias, conv_ln_g, conv_ln_b, w_conv_out, b_conv_out, w_o, norm_mix, norm_ffn,
           w_ffn_in, w_ffn_out, norm_final):
    f = lambda a: np.asarray(a, dtype=np.float32)
    x_prompt, x_sample, cache_k, cache_v, state_conv, meta_tokens = map(f, (x_prompt, x_sample, cache_k, cache_v, state_conv, meta_tokens))
    w_in, w_attn_out, w_conv_out, w_o, w_ffn_in, w_ffn_out = map(f, (w_in, w_attn_out, w_conv_out, w_o, w_ffn_in, w_ffn_out))
    L = DEPTH
    LW = RUN_DEPTH
    w_in, w_attn_out, w_conv_out, w_o, w_ffn_in, w_ffn_out = (a[:LW] for a in (w_in, w_attn_out, w_conv_out, w_o, w_ffn_in, w_ffn_out))
    win_r = _blockify(w_in[:, :, _win_cols()])
    wao_r = _blockify(w_attn_out)
    wco_r = _blockify(w_conv_out)
    wo_r = _blockify(w_o)
    f1cols = np.concatenate([np.concatenate([np.arange(128 * j, 128 * j + 128), np.arange(5632 + 128 * j, 5632 + 128 * j + 128)])
                             for j in range(44)])
    wf1_r = _blockify(w_ffn_in[:, :, f1cols])
    wf2_r = np.ascontiguousarray(w_ffn_out.reshape(LW, 4, 11, 128, 16, 128).transpose(0, 1, 4, 3, 2, 5)).reshape(LW, 4, 16, 128, 11 * 128)
    pvec = np.zeros((128, NV), np.float32)
    fm = lambda v, n: np.asarray(v, np.float32).reshape(n, 128).T
    for l in range(L):
        b = l * PV_L
        pvec[:, b:b + 16] = fm(norm_mix[l], 16)
        pvec[:, b + 16:b + 32] = fm(norm_ffn[l], 16)
        pvec[:, b + 32:b + 48] = fm(b_conv_out[l], 16)
        pvec[:, b + 48:b + 56] = fm(conv_dw_bias[l], 8)
        pvec[:, b + 56:b + 64] = fm(conv_ln_g[l], 8)
        pvec[:, b + 64:b + 72] = fm(conv_ln_b[l], 8)
        dw = np.asarray(conv_dw[l], np.float32)
        pvec[:, b + 72:b + 320] = dw.reshape(31, 8, 128).transpose(2, 1, 0).reshape(128, 248)
        pvec[:, PV_SINK + 16 * l:PV_SINK + 16 * l + 16] = np.asarray(attn_sinks[l], np.float32)[None, :]
    pvec[:, PV_FIN:PV_FIN + 16] = fm(norm_final, 16)
    identf = np.eye(128, dtype=np.float32)
    rmat = np.zeros((128, 128), np.float32)
    for base in (0, 64):
        for d in range(8):
            rmat[base + d + 8, base + d] = -1.0
            rmat[base + d, base + d + 8] = 1.0
    ii = np.arange(128)[:, None]
    jj = np.arange(128)[None, :]
    m_prev = np.where(jj >= ii, 0.0, NEG).astype(np.float32)
    m_own = np.where(jj <= ii, 0.0, NEG).astype(np.float32)
    m_std = np.concatenate([m_prev, m_own], 1)
    colv = (jj >= 112)
    m3 = np.concatenate([np.full((128, 128), NEG, np.float32), np.where((jj <= ii) & colv, 0.0, NEG).astype(np.float32)], 1)
    m4 = np.concatenate([np.where((jj >= ii) & colv, 0.0, NEG).astype(np.float32), m_own], 1)
    spos = 16384 + np.tile(np.arange(TS), NSEQ_S)
    shared = dict(win_r=win_r, wao_r=wao_r, wco_r=wco_r, wo_r=wo_r, wf1_r=wf1_r, wf2_r=wf2_r, pvec=pvec, identf=identf, rmat=rmat)
    in_maps = []
    for core in RUN_CORES:
        sq, ci = core // 4, core % 4
        xin = np.zeros((12 * 128 + 32, D), np.float32)
        pos = np.zeros((12 * 128,), np.int64)
        valid = np.ones((128, 512), np.float32)
        for s in range(12):
            tile = ci * 8 + (s - 4)
            if tile >= 0:
                xin[128 * s:128 * s + 128] = x_prompt[sq, 128 * tile:128 * tile + 128]
                pos[128 * s:128 * s + 128] = 16 + 128 * tile + np.arange(128)
            elif tile == -1:
                xin[128 * s + 112:128 * s + 128] = meta_tokens
                pos[128 * s + 112:128 * s + 128] = np.arange(16)
                valid[:, 128 * s:128 * s + 112] = 0.0
            else:
                valid[:, 128 * s:128 * s + 128] = 0.0
        xin[1536:1568] = x_sample[4 * core:4 * core + 4].reshape(32, D)
        rope = np.zeros((3, 2, 128, 544), np.float32)
        for g in range(3):
            p = pos[512 * g:512 * g + 512]
            p = np.concatenate([p, spos if g == 0 else np.zeros(32, np.int64)])
            rope[g, 0], rope[g, 1] = _rope_tables(p)
        masks = np.stack([m_std, m3 if ci == 0 else m_std, m4 if ci == 0 else m_std]).astype(np.float32)
        m = dict(shared)
        m.update(xin=xin, rope=rope, masks=masks, valid=valid,
                 ck=np.ascontiguousarray(cache_k[:, 4 * core:4 * core + 4].reshape(L, 4, 128, 256)),
                 cv=np.ascontiguousarray(cache_v[:, 4 * core:4 * core + 4].reshape(L, 4, 128, 256)),
                 sc=np.ascontiguousarray(state_conv[:, 4 * core:4 * core + 4]))
        in_maps.append(m)
    nc = _get_prog()
    res = run_bass_kernel_spmd(nc, in_maps, core_ids=list(range(len(RUN_CORES)))).results
    y_prompt = np.zeros((2, SEQ, D), np.float32)
    y_sample = np.zeros((32, TS, D), np.float32)
    nk_p = np.zeros((L, 2, 128, 4, 64), np.float32)
    nv_p = np.zeros((L, 2, 128, 4, 64), np.float32)
    nc_p = np.zeros((L, 2, 30, 1024), np.float32)
    nk_s = np.zeros((L, 32, 128, 4, 64), np.float32)
    nv_s = np.zeros((L, 32, 128, 4, 64), np.float32)
    nc_s = np.zeros((L, 32, 30, 1024), np.float32)
    for ri, core in enumerate(RUN_CORES):
        sq, ci = core // 4, core % 4
        r = res[ri]
        y_prompt[sq, 1024 * ci:1024 * ci + 1024] = r["y"][0:1024]
        y_sample[4 * core:4 * core + 4] = r["y"][1024:1056].reshape(4, TS, D)
        nk_s[:, 4 * core:4 * core + 4] = r["kso"].reshape(L, 4, 128, 4, 64)
        nv_s[:, 4 * core:4 * core + 4] = r["vso"].reshape(L, 4, 128, 4, 64)
        nc_s[:, 4 * core:4 * core + 4] = r["cso"]
        if ci == 3:
            nk_p[:, sq] = r["klast"].reshape(L, 128, 4, 64)
            nv_p[:, sq] = r["vlast"].reshape(L, 128, 4, 64)
            nc_p[:, sq] = r["clast"]
    return (y_prompt, y_sample, nk_p, nv_p, nc_p, nk_s, nv_s, nc_s)
```

```python
import contextlib
import numpy as np
import concourse.bass as bass
import concourse.mybir as mybir
from concourse.bass_utils import run_bass_kernel_spmd

F32 = mybir.dt.float32
BF16 = mybir.dt.bfloat16
ALU = mybir.AluOpType
AF = mybir.ActivationFunctionType
AX = mybir.AxisListType

D = 2048
NCH = 16
DEPTH = 4
SEQ = 4096
NSEQ_S = 4
TS = 8
NEG = -30000.0
EPS = 1e-6
SCALE = 0.125
PV_L = 320
PV_FIN = DEPTH * PV_L
PV_SINK = PV_FIN + 16
NV = PV_SINK + DEPTH * 16
NWSLOT = 5
NBLK_L = 262
USE_SCRATCH = False
NTMP = 4


class Tok:
    __slots__ = ("sem", "val", "eng")

    def __init__(self, eng):
        self.sem = None
        self.val = None
        self.eng = eng


class Sched:
    ENGS = ("pe", "act", "dve", "pool", "sp")
    EPOCH = 20000

    def __init__(self, nc):
        self.nc = nc
        self.ops = {e: [] for e in self.ENGS}
        self.cnt = {}
        self.last_w = {}
        self.readers = {}
        self.pending = {e: [] for e in self.ENGS}
        self.nsig = {e: 0 for e in self.ENGS}
        self.last_dma = {}
        self.out_toks = []

    @staticmethod
    def _is_psum(k):
        return (isinstance(k, tuple) and k[0] == "pj") or k in ("pS", "pTO")

    def op(self, eng, fn, reads=(), writes=(), dma=None, signal=True, extra=()):
        writes = list(writes) + [r for r in reads if self._is_psum(r) and r not in writes]
        deps = []
        for r in reads:
            t = self.last_w.get(r)
            if t is not None:
                deps.append(t)
        for w in writes:
            t = self.last_w.get(w)
            if t is not None:
                deps.append(t)
            deps.extend(self.readers.get(w, ()))
        deps.extend(extra)
        tok = Tok(eng if dma is None else "dma")
        if dma is not None:
            t = self.last_dma.get(dma)
            if t is not None:
                deps.append(t)
            tok.sem = "dma_" + dma
            self.cnt[tok.sem] = self.cnt.get(tok.sem, 0) + 16
            tok.val = self.cnt[tok.sem]
            self.last_dma[dma] = tok
            inc = (tok.sem, 16)
        elif signal:
            ep = self.nsig[eng] // self.EPOCH
            self.nsig[eng] += 1
            tok.sem = "%s_%d" % (eng, ep)
            self.cnt[tok.sem] = self.cnt.get(tok.sem, 0) + 1
            tok.val = self.cnt[tok.sem]
            for p in self.pending[eng]:
                p.sem, p.val = tok.sem, tok.val
            self.pending[eng] = []
            inc = (tok.sem, 1)
        else:
            self.pending[eng].append(tok)
            inc = None
        waits = []
        seen = set()
        for t in deps:
            if id(t) in seen:
                continue
            seen.add(id(t))
            if eng in ("pe", "act") and t.eng == eng:
                continue
            waits.append(t)
        self.ops[eng].append((fn, waits, inc))
        for r in reads:
            self.readers.setdefault(r, []).append(tok)
        for w in writes:
            self.last_w[w] = tok
            self.readers[w] = []
        return tok

    def build(self):
        nc = self.nc
        for e in self.ENGS:
            assert not self.pending[e], "unsignalled tail ops on " + e
        sems = {name: nc.alloc_semaphore(name) for name in sorted(self.cnt)}
        eng_attr = {"pe": "tensor", "act": "scalar", "dve": "vector", "pool": "gpsimd", "sp": "sync"}
        with nc.Block() as block:
            for e in self.ENGS:
                ops = self.ops[e]

                def body(engine, ops=ops):
                    waited = {}
                    for fn, waits, inc in ops:
                        for t in waits:
                            assert t.sem is not None
                            if waited.get(t.sem, 0) >= t.val:
                                continue
                            waited[t.sem] = t.val
                            engine.wait_ge(sems[t.sem], t.val)
                        if fn is None:
                            continue
                        ins = fn(engine)
                        if inc is not None:
                            ins.then_inc(sems[inc[0]], inc[1])

                getattr(block, eng_attr[e])(body)


class Prog:
    def __init__(self, depth=DEPTH, ngroups=3):
        self.depth = depth
        self.ngroups = ngroups
        self.nc = bass.Bass("TRN2", target_bir_lowering=False)
        self.S = Sched(self.nc)
        self.es = contextlib.ExitStack()
        self.rot = {}

    def dram(self, name, shape, kind):
        return self.nc.dram_tensor(name, list(shape), F32, kind=kind).ap()

    def sb(self, name, shape, dt):
        return self.es.enter_context(self.nc.sbuf_tensor("s_" + name, list(shape), dt))

    def ps(self, name, shape, dt):
        return self.es.enter_context(self.nc.psum_tensor("p_" + name, list(shape), dt))

    def nxt(self, what, n):
        i = self.rot.get(what, 0)
        self.rot[what] = i + 1
        return i % n

    def pj(self):
        i = self.nxt("pj", 6)
        return self.pjb[i], ("pj", i)

    def tmp(self):
        i = self.nxt("tmp", NTMP)
        return self.tmps[i], ("t", i)

    def PE(self, fn, r=(), w=(), sig=True):
        return self.S.op("pe", fn, r, w, signal=sig)

    def ACT(self, fn, r=(), w=()):
        return self.S.op("act", fn, r, w)

    def DVE(self, fn, r=(), w=()):
        return self.S.op("dve", fn, r, w)

    def act_copy(self, out, in_, r, w, func=AF.Copy, **kw):
        return self.ACT(lambda e: e.activation(out=out, in_=in_, func=func, **kw), r, w)

    def dve_copy(self, out, in_, r, w):
        return self.DVE(lambda e: e.tensor_copy(out=out, in_=in_), r, w)

    def tt(self, out, in0, in1, op, r, w):
        return self.DVE(lambda e: e.tensor_tensor(out=out, in0=in0, in1=in1, op=op), r, w)

    def ts(self, out, in0, s1, s2, op0, op1, r, w):
        if s2 is None:
            return self.DVE(lambda e: e.tensor_scalar(out=out, in0=in0, scalar1=s1, scalar2=None, op0=op0), r, w)
        return self.DVE(lambda e: e.tensor_scalar(out=out, in0=in0, scalar1=s1, scalar2=s2, op0=op0, op1=op1), r, w)

    def stt(self, out, in0, sc, in1, op0, op1, r, w):
        return self.DVE(lambda e: e.scalar_tensor_tensor(out=out, in0=in0, scalar=sc, in1=in1, op0=op0, op1=op1), r, w)

    def mm(self, out, lhsT, rhs, start, stop, r, w, sig=None):
        if DBG == "dmaonly":
            return None
        return self.PE(lambda e: e.matmul(out, lhsT=lhsT, rhs=rhs, start=start, stop=stop), r, w,
                       sig=(stop if sig is None else sig))

    def tr(self, out, in_, ident, r, w, sig):
        return self.PE(lambda e: e.transpose(out=out, in_=in_, identity=ident), r, w, sig=sig)

    def dma_in(self, out, in_, w, key, r=()):
        return self.S.op("sp", lambda e: e.dma_start(out=out, in_=in_), r, w, dma=key)

    def dma_out(self, out, in_, r):
        key = "out%d" % self.nxt("outk", 8)
        t = self.S.op("sp", lambda e: e.dma_start(out=out, in_=in_), r, (), dma=key)
        self.S.out_toks.append(t)
        return t

    def load_w(self, src, nk, ncols):
        i = self.nxt("w", NWSLOT)
        slot = self.wsl[i]
        n = nk * ncols
        dst = slot[:, 0:n]
        bid = self.bctr
        self.bctr += 1
        sc = self.wscs[bid // NBLK_L][bid % NBLK_L, :, 0:n]
        if self.cur_g == 0 or not USE_SCRATCH:
            self.S.op("pool", lambda e: e.dma_start(out=dst, in_=src), (), [("w", i)], dma="w%d" % i)
            if USE_SCRATCH and self.ngroups > 1:
                self.S.op("sp", lambda e: e.dma_start(out=sc, in_=dst), [("w", i)], [("wsc", bid)], dma="wb%d" % i)
        else:
            self.S.op("pool", lambda e: e.dma_start(out=dst, in_=sc), [("wsc", bid)], [("w", i)], dma="w%d" % i)
        return dst.rearrange("p (k n) -> p k n", k=nk), ("w", i)

    def rsqrt_inplace(self, ap, key, src, srckey):
        self.ts(ap, src, EPS, None, ALU.add, None, [srckey], [key])
        self.act_copy(ap, ap, [key], [key], func=AF.Sqrt)
        self.DVE(lambda e: e.reciprocal(out=ap, in_=ap), [key], [key])

    def build(self):
        with self.es:
            self._alloc()
            self._consts()
            for g in range(self.ngroups):
                if STOP == "consts":
                    break
                self.group(g)
            self.S.op("sp", None, extra=list(self.S.out_toks), signal=False)
            self.S.pending["sp"] = []
            self.S.build()
        return self.nc

    def _alloc(self):
        dp = self.depth
        I, O = "ExternalInput", "ExternalOutput"
        self.xin = self.dram("xin", [12 * 128 + 32, D], I)
        self.ck = self.dram("ck", [DEPTH, NSEQ_S, 128, 256], I)
        self.cv = self.dram("cv", [DEPTH, NSEQ_S, 128, 256], I)
        self.sc = self.dram("sc", [DEPTH, NSEQ_S, 30, 1024], I)
        self.pvec_d = self.dram("pvec", [128, NV], I)
        self.rope_d = self.dram("rope", [3, 2, 128, 544], I)
        self.masks_d = self.dram("masks", [3, 128, 256], I)
        self.valid_d = self.dram("valid", [128, 512], I)
        self.ident_d = self.dram("identf", [128, 128], I)
        self.rm_d = self.dram("rmat", [128, 128], I)
        self.win = self.dram("win_r", [dp, 62, 128, 16 * 128], I)
        self.wao = self.dram("wao_r", [dp, 16, 128, 8 * 128], I)
        self.wco = self.dram("wco_r", [dp, 16, 128, 8 * 128], I)
        self.wo = self.dram("wo_r", [dp, 16, 128, 16 * 128], I)
        self.wf1 = self.dram("wf1_r", [dp, 88, 128, 16 * 128], I)
        self.wf2 = self.dram("wf2_r", [dp, 4, 16, 128, 11 * 128], I)
        self.wscs = [self.nc.dram_tensor("wsc%d" % l, [NBLK_L, 128, 2048], BF16, kind="Internal").ap() for l in range(dp)]
        self.y = self.dram("y", [1024 + 32, D], O)
        self.klast = self.dram("klast", [DEPTH, 128, 256], O)
        self.vlast = self.dram("vlast", [DEPTH, 128, 256], O)
        self.clast = self.dram("clast", [DEPTH, 30, 1024], O)
        self.kso = self.dram("kso", [DEPTH, NSEQ_S, 128, 256], O)
        self.vso = self.dram("vso", [DEPTH, NSEQ_S, 128, 256], O)
        self.cso = self.dram("cso", [DEPTH, NSEQ_S, 30, 1024], O)

        sb = self.sb
        T = 544
        self.xT = sb("xT", [128, 16, T], F32)
        self.hT = sb("hT", [128, 16, T], BF16)
        self.qa = sb("qa", [128, 8, T], BF16)
        self.oT = sb("oT", [128, 8, T], BF16)
        self.kbl = sb("kbl", [128, 4, 128 + T], BF16)
        self.kbh = sb("kbh", [128, 4, 128 + T], BF16)
        self.vb = sb("vb", [128, 2, 128 + T], BF16)
        self.kf = sb("kf", [128, 4, 160], F32)
        self.vf = sb("vf", [128, 2, 160], F32)
        self.u = sb("u", [128, 8, T], BF16)
        self.uo = sb("uo", [128, 8, 32], F32)
        self.dg = [sb("dg%d" % i, [128, 128], BF16) for i in range(6)]
        self.us = sb("us", [128, 8, NSEQ_S, 38], F32)
        self.yc = sb("yc", [128, 8, T], F32)
        ycf = self.yc[:].rearrange("p c t -> p (c t)")
        self.mg = ycf.bitcast(BF16).rearrange("p (c t) -> p c t", c=16)
        self.fa = ycf.bitcast(BF16).rearrange("p (c t) -> p c t", c=16)
        self.fin = ycf[:, 0:2048].rearrange("p (c t) -> p c t", c=16)
        self.rstd = sb("rstd", [128, T], F32)
        self.tmps = [sb("tmp%d" % i, [128, 512], F32) for i in range(NTMP)]
        self.tbs = [sb("tb%d" % i, [128, 512], BF16) for i in range(1)]
        self.wsl = [sb("wsl%d" % i, [128, 2048], BF16) for i in range(NWSLOT)]
        self.vtl = [sb("vtl%d" % i, [128, 4, 128], BF16) for i in range(3)]
        self.vth = [sb("vth%d" % i, [128, 4, 128], BF16) for i in range(3)]
        self.kcl = sb("kcl", [128, 4, 128], BF16)
        self.kch = sb("kch", [128, 4, 128], BF16)
        self.vcl = sb("vcl", [128, 4, 128], BF16)
        self.vch = sb("vch", [128, 4, 128], BF16)
        self.vol = sb("vol", [128, 4, 128], BF16)
        self.voh = sb("voh", [128, 4, 128], BF16)
        self.pn = [sb("pn%d" % i, [128, 2, 256], BF16) for i in range(2)]
        self.ptsb = [sb("ptsb%d" % i, [128, 4, 128], BF16) for i in range(2)]
        self.st = [sb("st%d" % i, [128, 16], F32) for i in range(2)]
        self.masks = sb("masks", [128, 3, 256], F32)
        self.Ct = sb("Ct", [128, T], F32)
        self.St = sb("St", [128, T], F32)
        self.valid = sb("valid", [128, 512], F32)
        self.pvec = sb("pvecs", [128, NV], F32)
        self.identf = sb("identfs", [128, 128], F32)
        self.identb = sb("identb", [128, 128], BF16)
        self.rmf = sb("rmf", [128, 128], F32)
        self.rmb = sb("rmb", [128, 128], BF16)
        self.ones_r = sb("ones_r", [128, 128], F32)
        self.ones_l = sb("ones_l", [128, 128], F32)
        self.cK = [sb("cK%d" % l, [128, 4, 128], BF16) for l in range(dp)]
        self.cV = [sb("cV%d" % l, [128, 2, 128], BF16) for l in range(dp)]
        self.cU = [sb("cU%d" % l, [128, 8, 30], BF16) for l in range(dp)]
        self.xs = sb("xs", [128, D], F32)
        self.pjb = [self.ps("pj%d" % i, [128, 512], F32) for i in range(6)]
        self.pS = self.ps("pS", [128, 2, 256], F32)
        pto = self.ps("pTO", [128, 512], F32)
        self.pO = pto[:, 0:256]
        self.pT = pto[:, 256:512].bitcast(BF16).rearrange("p (j t) -> p j t", j=4)
        self.laneS = [(self.pS, "pS"), (self.pjb[4][:].rearrange("p (h k) -> p h k", h=2), ("pj", 4))]
        self.laneT = [(self.pT, "pTO"), (self.pjb[5][:, 256:512].bitcast(BF16).rearrange("p (j t) -> p j t", j=4), ("pj", 5))]
        self.laneO = [(self.pO, "pTO"), (self.pjb[5][:, 0:256], ("pj", 5))]

    def _consts(self):
        S = self.S
        toks = []
        toks.append(self.dma_in(self.pvec[:], self.pvec_d, ["pvec"], "c0"))
        toks.append(self.dma_in(self.masks[:], self.masks_d.rearrange("m p k -> p m k"), ["masks"], "c1"))
        toks.append(self.dma_in(self.valid[:], self.valid_d, ["valid"], "c2"))
        toks.append(self.dma_in(self.identf[:], self.ident_d, ["identf"], "c3"))
        toks.append(self.dma_in(self.rmf[:], self.rm_d, ["rmf"], "c4"))
        self.dve_copy(self.identb[:], self.identf[:], ["identf"], ["identb"])
        self.dve_copy(self.rmb[:], self.rmf[:], ["rmf"], ["rmb"])
        self.DVE(lambda e: e.memset(self.ones_r[:], 1.0 / 2048.0), (), ["ones_r"])
        self.DVE(lambda e: e.memset(self.ones_l[:], 1.0 / 1024.0), (), ["ones_l"])
        for i in range(3):
            self.DVE(lambda e, i=i: e.memset(self.vtl[i][:], 0.0), (), [("vt", i)])
            self.DVE(lambda e, i=i: e.memset(self.vth[i][:], 0.0), (), [("vt", i)])
        for t_, k in ((self.vcl, "vc"), (self.vch, "vc"), (self.vol, "vo"), (self.voh, "vo")):
            self.DVE(lambda e, t_=t_: e.memset(t_[:], 0.0), (), [k])
        self.DVE(lambda e: e.memset(self.u[:], 0.0), (), [("u", c) for c in range(8)])
        self.DVE(lambda e: e.memset(self.kbl[:], 0.0), (), [("kb", g) for g in range(4)])
        self.DVE(lambda e: e.memset(self.kbh[:], 0.0), (), [("kb", g) for g in range(4)])
        self.DVE(lambda e: e.memset(self.kcl[:], 0.0), (), ["kc"])
        self.DVE(lambda e: e.memset(self.kch[:], 0.0), (), ["kc"])
        self.DVE(lambda e: e.memset(self.vb[:], 0.0), (), [("vb", c) for c in range(2)])
        last = self.DVE(lambda e: e.memset(self.rstd[:], 0.0), (), ["rstd"])
        for eng in ("pe", "act", "dve", "pool", "sp"):
            S.op(eng, None, extra=toks + [last], signal=False)
            S.pending[eng] = []

    def pieces(self, g, c0):
        out = []
        if c0 < 512:
            out.append((c0, 512))
        if g == 0:
            out.append((512, 544))
        return out

    def group(self, g):
        self.dma_in(self.Ct[:], self.rope_d[g, 0], ["Ct"], "rope")
        self.dma_in(self.St[:], self.rope_d[g, 1], ["St"], "rope2")
        loads = [(4 * g + s, 128, 128 * s) for s in range(4)]
        if g == 0:
            loads.append((12, 32, 512))
        for (slot, n, c0) in loads:
            self.dma_in(self.xs[0:n, :], self.xin[slot * 128:slot * 128 + n, :], ["xs"], "xin")
            for c4 in range(4):
                bank, bk = self.pj()
                bv = bank[:].rearrange("p (j t) -> p j t", j=4)
                for j in range(4):
                    c = c4 * 4 + j
                    self.tr(bv[:, j, 0:n], self.xs[0:n, c * 128:(c + 1) * 128], self.identf[0:n, 0:n], ["xs"], [bk], j == 3)
                self.act_copy(self.xT[:, c4 * 4:(c4 + 1) * 4, c0:c0 + n], bv[:, :, 0:n], [bk], [("x", c) for c in range(c4 * 4, c4 * 4 + 4)])
        if STOP == "load":
            return
        for l in range(self.depth):
            self.layer(g, l)
        if STOP == "nofinal":
            return
        self.final(g)

    def rmsnorm(self, pcs, gcol, out_fn, okeys):
        for (a, b) in pcs:
            n = b - a
            bank, bk = self.pj()
            for c in range(NCH):
                t, tk = self.tmp()
                self.act_copy(t[:, 0:n], self.xT[:, c, a:b], [("x", c)], [tk], func=AF.Square)
                self.mm(bank[:, 0:n], self.ones_r[:], t[:, 0:n], c == 0, c == NCH - 1, [tk], [bk], sig=True)
            self.rsqrt_inplace(self.rstd[:, a:b], "rstd", bank[:, 0:n], bk)
            for c in range(NCH):
                self.stt(out_fn(c, a, b), self.xT[:, c, a:b], self.pvec[:, gcol + c:gcol + c + 1], self.rstd[:, a:b],
                         ALU.mult, ALU.mult, [("x", c), "rstd"], [okeys(c)])

    def proj(self, wv, wk, nk, rhs_fn, rkeys, a, b):
        n = b - a
        bank, bk = self.pj()
        for k in range(nk):
            self.mm(bank[:, 0:n], wv[:, k, :], rhs_fn(k, a, b), k == 0, k == nk - 1, [wk, rkeys(k)], [bk])
        return bank, bk

    def layer(self, g, l):
        self.bctr = l * NBLK_L
        self.cur_g = g
        pb = l * PV_L
        c_ext = 128 * l if g == 0 else 0
        c_full = 128 * (l + 1) if g == 0 else 0
        p_ext = self.pieces(g, c_ext)
        p_full = self.pieces(g, c_full)
        hrhs = lambda k, a, b: self.hT[:, k, a:b]
        hkey = lambda k: ("h", k)
        if g > 0:
            self.act_copy(self.kbl[0:64, :, 0:128], self.cK[l][0:64], [("cK", l)], [("kb", i) for i in range(4)])
            self.act_copy(self.kbh[64:128, :, 0:128], self.cK[l][64:128], [("cK", l)], [("kb", i) for i in range(4)])
            self.act_copy(self.vb[:, :, 0:128], self.cV[l][:], [("cV", l)], [("vb", i) for i in range(2)])
            self.dve_copy(self.u[:, :, 0:30], self.cU[l][:], [("cU", l)], [("u", c) for c in range(8)])
        else:
            for j in range(NSEQ_S):
                self.dma_in(self.xs[0:30, 0:1024], self.sc[l, j], ["xs"], "cst")
                for hf in range(2):
                    bank, bk = self.pj()
                    bv = bank[:].rearrange("p (j t) -> p j t", j=4)
                    for cc in range(4):
                        c = hf * 4 + cc
                        self.tr(bv[:, cc, 0:30], self.xs[0:30, c * 128:(c + 1) * 128], self.identf[0:30, 0:30], ["xs"], [bk], cc == 3)
                    self.act_copy(self.us[:, hf * 4:(hf + 1) * 4, j, 0:30], bv[:, :, 0:30], [bk], ["us"])
        if STOP == "carry":
            return
        self.rmsnorm(p_ext, pb + 0, lambda c, a, b: self.hT[:, c, a:b], hkey)
        if STOP == "norm1":
            return
        for gi in range(4):
            wv, wk = self.load_w(self.win[l, 8 + gi], 16, 128)
            if DBG == "nomm":
                self.act_copy(self.tbs[0][:, 0:128], wv[:, 0, :], [wk], [("tb", 0)])
                continue
            for (a, b) in p_ext:
                if DBG == "p512" and a >= 512:
                    continue
                if DBG == "p32" and a < 512:
                    continue
                bank, bk = self.proj(wv, wk, 16, hrhs, hkey, a, b)
                dests = [(self.kbl[0:64, gi, 128 + a:128 + b], ("kb", gi), 0, (0, 64)),
                         (self.kbh[64:128, gi, 128 + a:128 + b], ("kb", gi), 0, (64, 128))]
                if b > 384:
                    lo = max(a, 384)
                    dests.append((self.kf[:, gi, lo - 384:b - 384], "kf", lo - a))
                if DBG in ("norot", "p512", "p32", "dvesecond"):
                    self.act_copy(dests[0][0], bank[0:64, 0:b - a], [bk], [dests[0][1]])
                elif DBG == "noproj":
                    pass
                else:
                    self.rot_evac(bank, bk, a, b, dests)
        if STOP == "kproj":
            return
        for c in range(2):
            wv, wk = self.load_w(self.win[l, 12 + c], 16, 128)
            for (a, b) in p_ext:
                n = b - a
                bank, bk = self.proj(wv, wk, 16, hrhs, hkey, a, b)
                self.act_copy(self.vb[:, c, 128 + a:128 + b], bank[:, 0:n], [bk], [("vb", c)])
                if b > 384:
                    lo = max(a, 384)
                    self.dve_copy(self.vf[:, c, lo - 384:b - 384], bank[:, lo - a:n], [bk], ["vf"])
        if STOP == "vproj":
            return
        for c in range(8):
            wv1, wk1 = self.load_w(self.win[l, 14 + 2 * c], 16, 128)
            wv2, wk2 = self.load_w(self.win[l, 15 + 2 * c], 16, 128)
            for (a, b) in p_ext:
                n = b - a
                b1, k1 = self.proj(wv1, wk1, 16, hrhs, hkey, a, b)
                b2, k2 = self.proj(wv2, wk2, 16, hrhs, hkey, a, b)
                t, tk = self.tmp()
                self.act_copy(t[:, 0:n], b2[:, 0:n], [k2], [tk], func=AF.Sigmoid)
                if a < 512:
                    self.tt(self.u[:, c, 30 + a:30 + b], b1[:, 0:n], t[:, 0:n], ALU.mult, [k1, tk], [("u", c)])
                    if g == self.ngroups - 1:
                        self.tt(self.uo[:, c, 0:30], b1[:, 482 - a:512 - a], t[:, 482 - a:512 - a], ALU.mult, [k1, tk], ["uo"])
                else:
                    self.tt(self.us[:, c, :, 30:38], b1[:, 0:32].rearrange("p (s t) -> p s t", s=NSEQ_S),
                            t[:, 0:32].rearrange("p (s t) -> p s t", s=NSEQ_S), ALU.mult, [k1, tk], ["us"])
        if g == 0:
            ukeys = [("u", c) for c in range(8)]
            if c_ext < 512:
                nv_ = 512 - c_ext
                self.tt(self.u[:, :, 30 + c_ext:542], self.u[:, :, 30 + c_ext:542],
                        self.valid[:, c_ext:512].unsqueeze(1).to_broadcast([128, 8, nv_]), ALU.mult, ukeys, ukeys)
        if STOP == "glu":
            return
        for c in range(8):
            wv, wk = self.load_w(self.win[l, c], 16, 128)
            for (a, b) in p_full:
                bank, bk = self.proj(wv, wk, 16, hrhs, hkey, a, b)
                self.rot_evac(bank, bk, a, b, [(self.qa[:, c, a:b], ("qa", c))])
        if STOP == "qproj":
            return
        if g < self.ngroups - 1:
            self.act_copy(self.cK[l][0:64], self.kbl[0:64, :, 512:640], [("kb", i) for i in range(4)], [("cK", l)])
            self.act_copy(self.cK[l][64:128], self.kbh[64:128, :, 512:640], [("kb", i) for i in range(4)], [("cK", l)])
            self.act_copy(self.cV[l][:], self.vb[:, :, 512:640], [("vb", i) for i in range(2)], [("cV", l)])
            self.dve_copy(self.cU[l][:], self.u[:, :, 512:542], [("u", c) for c in range(8)], [("cU", l)])
        if g == self.ngroups - 1:
            self.out_rows(l, 128, 0, 482, self.klast[l], self.vlast[l], self.clast[l])
        if g == 0:
            self.out_rows(l, 32, 128, None, None, None, None)
        if STOP == "inproj":
            return
        self.deferred = self.conv_taps(g, l, p_full)
        n_units = 8 * max(0, 4 - c_full // 128) + (8 * NSEQ_S if g == 0 else 0)
        self.per_unit = -(-len(self.deferred) // max(1, n_units))
        self.attention(g, l, c_full)
        if STOP == "attn":
            return
        self.conv(g, l, p_full)
        for c in range(NCH):
            wa, ka = self.load_w(self.wao[l, c], 8, 128)
            wc, kc_ = self.load_w(self.wco[l, c], 8, 128)
            wga, kga = self.load_w(self.win[l, 30 + 2 * c], 16, 128)
            wgb, kgb = self.load_w(self.win[l, 31 + 2 * c], 16, 128)
            for (a, b) in p_full:
                n = b - a
                pa, pak = self.proj(wa, ka, 8, lambda k, a, b: self.oT[:, k, a:b], lambda k: ("o", k), a, b)
                pga, pgak = self.proj(wga, kga, 16, hrhs, hkey, a, b)
                t1, t1k = self.tmp()
                self.act_copy(t1[:, 0:n], pga[:, 0:n], [pgak], [t1k], func=AF.Sigmoid)
                self.tt(t1[:, 0:n], pa[:, 0:n], t1[:, 0:n], ALU.mult, [pak, t1k], [t1k])
                pc, pck = self.proj(wc, kc_, 8, lambda k, a, b: self.qa[:, k, a:b], lambda k: ("qa", k), a, b)
                pgb, pgbk = self.proj(wgb, kgb, 16, hrhs, hkey, a, b)
                t2, t2k = self.tmp()
                self.act_copy(t2[:, 0:n], pgb[:, 0:n], [pgbk], [t2k], func=AF.Sigmoid)
                self.stt(t2[:, 0:n], pc[:, 0:n], self.pvec[:, pb + 32 + c:pb + 33 + c], t2[:, 0:n], ALU.add, ALU.mult,
                         [pck, t2k], [t2k])
                self.tt(self.mg[:, c, a:b], t1[:, 0:n], t2[:, 0:n], ALU.add, [t1k, t2k], [("yc", c // 2)])
        if STOP == "merge":
            return
        for c in range(NCH):
            wv, wk = self.load_w(self.wo[l, c], 16, 128)
            for (a, b) in p_full:
                n = b - a
                bank, bk = self.proj(wv, wk, 16, lambda k, a, b: self.mg[:, k, a:b], lambda k: ("yc", k // 2), a, b)
                self.tt(self.xT[:, c, a:b], self.xT[:, c, a:b], bank[:, 0:n], ALU.add, [bk, ("x", c)], [("x", c)])
        self.rmsnorm(p_full, pb + 16, lambda c, a, b: self.hT[:, c, a:b], hkey)
        for hg in range(4):
            for jj in range(11):
                j = hg * 11 + jj
                w1, k1 = self.load_w(self.wf1[l, 2 * j], 16, 128)
                w2, k2 = self.load_w(self.wf1[l, 2 * j + 1], 16, 128)
                for (a, b) in p_full:
                    n = b - a
                    b1, bk1 = self.proj(w1, k1, 16, hrhs, hkey, a, b)
                    b2, bk2 = self.proj(w2, k2, 16, hrhs, hkey, a, b)
                    t, tk = self.tmp()
                    self.act_copy(t[:, 0:n], b1[:, 0:n], [bk1], [tk], func=AF.Silu)
                    self.tt(self.fa[:, jj, a:b], t[:, 0:n], b2[:, 0:n], ALU.mult, [tk, bk2], [("yc", jj // 2)])
            for c in range(NCH):
                wv, wk = self.load_w(self.wf2[l, hg, c], 11, 128)
                for (a, b) in p_full:
                    n = b - a
                    bank, bk = self.proj(wv, wk, 11, lambda k, a, b: self.fa[:, k, a:b], lambda k: ("yc", k // 2), a, b)
                    self.tt(self.xT[:, c, a:b], self.xT[:, c, a:b], bank[:, 0:n], ALU.add, [bk, ("x", c)], [("x", c)])

    def rot_evac(self, bank, bk, a, b, dests):
        n = b - a
        i = self.nxt("tb", 1)
        tb, tbk = self.tbs[i], ("tb", i)
        self.act_copy(tb[:, 0:n], bank[:, 0:n], [bk], [tbk])
        t1, t1k = self.tmp()
        self.tt(t1[:, 0:n], bank[:, 0:n], self.Ct[:, a:b], ALU.mult, [bk, "Ct"] + ([tbk] if DBG == "rot1s" else []), [t1k])
        if DBG in ("rot1", "rot1s"):
            for d in dests:
                off = d[2] if len(d) > 2 else 0
                self.act_copy(d[0], t1[:, off:n], [t1k], [d[1]])
            return
        b2, b2k = self.pj()
        self.mm(b2[:, 0:n], self.rmb[:], tb[:, 0:n], True, True, [tbk], [b2k])
        t2, t2k = self.tmp()
        self.tt(t2[:, 0:n], b2[:, 0:n], self.St[:, a:b], ALU.mult, [b2k, "St"], [t2k])
        if DBG == "rot2":
            for d in dests:
                off = d[2] if len(d) > 2 else 0
                self.act_copy(d[0], t2[:, off:n], [t1k, t2k], [d[1]])
            return
        for d in dests:
            off = d[2] if len(d) > 2 else 0
            p0, p1 = d[3] if len(d) > 3 else (0, 128)
            self.tt(d[0], t1[p0:p1, off:n], t2[p0:p1, off:n], ALU.add, [t1k, t2k], [d[1]])

    def out_rows(self, l, n, kcol, ucol, kdst, vdst, cdst):
        bank, bk = self.pj()
        bv = bank[:].rearrange("p (j t) -> p j t", j=4)
        for gi in range(4):
            self.tr(bv[0:n, gi, :], self.kf[:, gi, kcol:kcol + n], self.identf[:], ["kf"], [bk], gi == 3)
        self.act_copy(self.xs[0:n, 0:256].rearrange("p (g d) -> p g d", g=4), bv[0:n, :, 0:64], [bk], ["xs"])
        bank, bk = self.pj()
        bv = bank[:].rearrange("p (j t) -> p j t", j=4)
        for c in range(2):
            self.tr(bv[0:n, c, :], self.vf[:, c, kcol:kcol + n], self.identf[:], ["vf"], [bk], c == 1)
        self.act_copy(self.xs[0:n, 256:512].rearrange("p (g d) -> p g d", g=2), bv[0:n, 0:2, :], [bk], ["xs"])
        m = 30 if n == 128 else 32
        for hf in range(2):
            if n != 128:
                tq, tqk = self.tmp()
                for cc in range(4):
                    self.dve_copy(tq[:, cc * 32:(cc + 1) * 32].rearrange("p (s t) -> p s t", s=NSEQ_S),
                                  self.us[:, hf * 4 + cc, :, 30:38], ["us"], [tqk])
            bank, bk = self.pj()
            bv = bank[:].rearrange("p (j t) -> p j t", j=4)
            for cc in range(4):
                c = hf * 4 + cc
                src = self.uo[:, c, 0:30] if n == 128 else tq[:, cc * 32:(cc + 1) * 32]
                rk = "uo" if n == 128 else tqk
                self.tr(bv[0:m, cc, :], src, self.identf[:], [rk], [bk], cc == 3)
            self.act_copy(self.xs[0:m, 1024 + hf * 512:1536 + hf * 512].rearrange("p (g d) -> p g d", g=4), bv[0:m, :, :], [bk], ["xs"])
        if n == 128:
            self.dma_out(kdst, self.xs[0:128, 0:256], ["xs"])
            self.dma_out(vdst, self.xs[0:128, 256:512], ["xs"])
            self.dma_out(cdst, self.xs[0:30, 1024:2048], ["xs"])
        else:
            self.dma_out(self.kso[l, :, 0:120, :], self.ck[l, :, 8:128, :], [])
            self.dma_out(self.vso[l, :, 0:120, :], self.cv[l, :, 8:128, :], [])
            self.dma_out(self.cso[l, :, 0:22, :], self.sc[l, :, 8:30, :], [])
            for j in range(NSEQ_S):
                self.dma_out(self.kso[l, j, 120:128, :], self.xs[8 * j:8 * j + 8, 0:256], ["xs"])
                self.dma_out(self.vso[l, j, 120:128, :], self.xs[8 * j:8 * j + 8, 256:512], ["xs"])
                self.dma_out(self.cso[l, j, 22:30, :], self.xs[8 * j:8 * j + 8, 1024:2048], ["xs"])

    def build_vt(self, i, col0):
        for c in range(2):
            self.tr(self.pT[:, c, :], self.vb[:, c, col0:col0 + 128], self.identb[:], [("vb", c)], ["pTO"], c == 1)
        src = self.pT[:, 0:2, :].rearrange("p c (h d) -> p (c h) d", h=2)
        self.act_copy(self.vtl[i][:, :, 0:64], src, ["pTO"], [("vt", i)])
        self.dve_copy(self.vth[i][:, :, 64:128], src, ["pTO"], [("vt", i)])

    def attention(self, g, l, c_full):
        s0 = c_full // 128
        prev = None
        for s in range(s0, 4):
            if prev is None:
                prev = self.nxt("vt", 3)
                self.build_vt(prev, 128 * s)
            own = self.nxt("vt", 3)
            self.build_vt(own, 128 + 128 * s)
            plist = []
            for un in range(8):
                kk4 = [("kb", i) for i in range(4)]
                segs = [((self.kbl[:, :, 128 * s:128 * s + 128], self.kbh[:, :, 128 * s:128 * s + 128]), kk4, self.vtl[prev], self.vth[prev], ("vt", prev), 128),
                        ((self.kbl[:, :, 128 + 128 * s:256 + 128 * s], self.kbh[:, :, 128 + 128 * s:256 + 128 * s]), kk4, self.vtl[own], self.vth[own], ("vt", own), 128)]
                mi = 0
                if g == 0 and s == 3:
                    mi = 1
                if g == 1 and s == 0:
                    mi = 2
                plist.append((l, un, 128, 128 * s, segs, mi))
            self.run_units(plist)
            prev = own
        if g == 0 and DBG not in ("vtonly", "nosample", "qk", "exp", "pn", "ptr"):
            for j in range(NSEQ_S):
                stg, sk_ = self.tmp()
                self.dma_in(stg[:, 0:256], self.ck[l, j], [sk_], "stg")
                src = stg[:, 0:256].rearrange("p (g d) -> p g d", g=4)
                stgd_, sdk = self.tmp()
                stgd = stgd_[:].rearrange("p (g d) -> p g d", g=4)
                self.act_copy(stgd[:, :, 0:64], src, [sk_], [sdk])
                self.dve_copy(stgd[:, :, 64:128], src, [sk_], [sdk])
                bank, bk = self.pj()
                bv = bank[:].rearrange("p (j t) -> p j t", j=4)
                for gi in range(4):
                    self.tr(bv[:, gi, :], stgd[:, gi, :], self.identf[:], [sdk], [bk], gi == 3)
                self.act_copy(self.kcl[0:64], bv[0:64], [bk], ["kc"])
                self.act_copy(self.kch[64:128], bv[64:128], [bk], ["kc"])
                stg, sk_ = self.tmp()
                self.dma_in(stg[:, 0:256], self.cv[l, j], [sk_], "stg")
                src = stg[:, 0:256].rearrange("p (g d) -> p g d", g=4)
                self.act_copy(self.vcl[:, :, 0:64], src, [sk_], ["vc"])
                self.dve_copy(self.vch[:, :, 64:128], src, [sk_], ["vc"])
                col = 128 + 512 + TS * j
                for c in range(2):
                    self.tr(self.pT[0:TS, c, :], self.vb[:, c, col:col + TS], self.identb[:], [("vb", c)], ["pTO"], c == 1)
                src = self.pT[0:TS, 0:2, :].rearrange("p c (h d) -> p (c h) d", h=2)
                self.act_copy(self.vol[0:TS, :, 0:64], src, ["pTO"], ["vo"])
                self.dve_copy(self.voh[0:TS, :, 64:128], src, ["pTO"], ["vo"])
                plist = []
                for un in range(8):
                    segs = [((self.kcl[:], self.kch[:]), ["kc"], self.vcl, self.vch, "vc", 128),
                            ((self.kbl[:, :, col:col + TS], self.kbh[:, :, col:col + TS]), [("kb", i) for i in range(4)], self.vol, self.voh, "vo", TS)]
                    plist.append((l, un, TS, 512 + TS * j, segs, 0))
                self.run_units(plist)

    def unit(self, l, un, nq, qc, segs, mi, lane):
        gk = un // 2
        nkt = sum(sg[5] for sg in segs)
        pS, pSk = self.laneS[lane]
        pT, pTk = self.laneT[lane]
        pO, pOk = self.laneO[lane]
        for h in range(2):
            off = 0
            for si, (kap, kkeys, vl, vh, vk, nk) in enumerate(segs):
                self.mm(pS[0:nq, h, off:off + nk], self.qa[:, un, qc:qc + nq], kap[h][:, gk, 0:nk], True, True,
                        [("qa", un), kkeys[gk % len(kkeys)]], [pSk], sig=(h == 1 and si == len(segs) - 1))
                off += nk
        yield
        t, tk = self.tmp()
        sm = t[:].rearrange("p (h k) -> p h k", h=2)
        st, stk = self.st[lane], ("st", lane)
        self.tt(sm[0:nq, :, 0:nkt], pS[0:nq, :, 0:nkt], self.masks[0:nq, mi:mi + 1, 0:nkt].to_broadcast([nq, 2, nkt]), ALU.add,
                [pSk], [tk])
        self.DVE(lambda e: e.tensor_reduce(out=st[0:nq, 0:2], in_=sm[0:nq, :, 0:nkt], axis=AX.X, op=ALU.max), [tk], [stk])
        self.ts(st[0:nq, 2:4], st[0:nq, 0:2], -100.0, -SCALE, ALU.max, ALU.mult, [stk], [stk])
        self.DVE(lambda e: e.memset(st[0:nq, 4:6], 0.0), [stk], [stk])
        yield
        for h in range(2):
            self.ACT(lambda e, h=h: e.activation(out=sm[0:nq, h, 0:nkt], in_=sm[0:nq, h, 0:nkt], func=AF.Exp, bias=st[0:nq, 2 + h:3 + h],
                                                  scale=SCALE, accum_out=st[0:nq, 4 + h:5 + h]), [tk, stk], [tk, stk])
        yield
        sk = PV_SINK + l * 16 + 2 * un
        self.tt(st[0:nq, 6:8], st[0:nq, 2:4], self.pvec[0:nq, sk:sk + 2], ALU.add, [stk], [stk])
        self.act_copy(st[0:nq, 6:8], st[0:nq, 6:8], [stk], [stk], func=AF.Exp)
        yield
        self.tt(st[0:nq, 8:10], st[0:nq, 4:6], st[0:nq, 6:8], ALU.add, [stk], [stk])
        self.DVE(lambda e: e.reciprocal(out=st[0:nq, 10:12], in_=st[0:nq, 8:10]), [stk], [stk])
        pn, pnk = self.pn[lane], ("pn", lane)
        self.tt(pn[0:nq, :, 0:nkt], sm[0:nq, :, 0:nkt], st[0:nq, 10:12].unsqueeze(2).to_broadcast([nq, 2, nkt]), ALU.mult,
                [tk, stk], [pnk])
        yield
        cnt_ = 0
        for h in range(2):
            off = 0
            for si, sg in enumerate(segs):
                nk = sg[5]
                cnt_ += 1
                self.tr(pT[0:nk, si * 2 + h, 0:nq], pn[0:nq, h, off:off + nk], self.identb[0:nq, 0:nq], [pnk], [pTk],
                        cnt_ == 2 * len(segs))
                off += nk
        ptsb, ptk = self.ptsb[lane], ("ptsb", lane)
        nks = [sg[5] for sg in segs]
        if len(set(nks)) == 1:
            self.act_copy(ptsb[0:nks[0], :, 0:nq], pT[0:nks[0], :, 0:nq], [pTk], [ptk])
        else:
            for si, nk in enumerate(nks):
                self.act_copy(ptsb[0:nk, 2 * si:2 * si + 2, 0:nq], pT[0:nk, 2 * si:2 * si + 2, 0:nq], [pTk], [ptk])
        yield
        idx = 0
        last = 2 * len(segs) - 1
        for h in range(2):
            for si, (kap, kkeys, vl, vh, vk, nk) in enumerate(segs):
                vsel = vl if h == 0 else vh
                self.mm(pO[:, 0:nq], vsel[0:nk, gk, :], ptsb[0:nk, si * 2 + h, 0:nq], idx == 0, idx == last, [vk, ptk], [pOk])
                idx += 1
        self.act_copy(self.oT[:, un, qc:qc + nq], pO[:, 0:nq], [pOk], [("o", un)])
        self.drain(self.per_unit)

    def run_units(self, plist):
        for i in range(0, len(plist), 2):
            gens = [self.unit(*plist[i], 0)]
            if i + 1 < len(plist):
                gens.append(self.unit(*plist[i + 1], 1))
            started = 0
            live = []
            step = 0
            while gens or live:
                if gens and (step == 0 or step == 2):
                    live.append(gens.pop(0))
                for gen in list(live):
                    try:
                        next(gen)
                    except StopIteration:
                        live.remove(gen)
                step += 1

    def conv_taps(self, g, l, p_full):
        pb = l * PV_L
        wcol = pb + 72
        ops = []
        pe_ops = []
        for c in range(8):
            for (a, b) in p_full:
                if a < 512:
                    bias = self.pvec[:, pb + 48 + c:pb + 49 + c]
                    holder = {}
                    for j in range(31):
                        wj = self.pvec[:, wcol + c * 31 + j:wcol + c * 31 + j + 1]

                        def pe_tap(j=j, wj=wj, c=c, a=a, b=b, holder=holder, bias=bias):
                            if j == 0:
                                bi = self.nxt("pjc", 4)
                                holder["bank"], holder["bk"] = self.pjb[bi], ("pj", bi)
                            bank, bk = holder["bank"], holder["bk"]
                            di = self.nxt("dg", 6)
                            dgt, dgk = self.dg[di], ("dg", di)
                            self.ts(dgt[:], self.identb[:], wj, None, ALU.mult, None, [], [dgk])
                            self.mm(bank[:, 0:b - a], dgt[:], self.u[:, c, a + j:b + j], j == 0, j == 30, [dgk, ("u", c)], [bk], sig=True)
                            if j == 30:
                                self.ts(self.yc[:, c, a:b], bank[:, 0:b - a], bias, None, ALU.add, None, [bk], [("yc", c)])
                        pe_ops.append(pe_tap)
                    continue
                else:
                    acc = self.yc[:, c, 512:544].rearrange("p (s t) -> p s t", s=NSEQ_S)
                    srcs = [self.us[:, c, :, j:j + TS] for j in range(31)]
                    rk = "us"
                for j in range(31):
                    wj = self.pvec[:, wcol + c * 31 + j:wcol + c * 31 + j + 1]
                    if j == 0:
                        ops.append(lambda acc=acc, s0=srcs[0], wj=wj, c=c, rk=rk: self.ts(
                            acc, s0, wj, self.pvec[:, pb + 48 + c:pb + 49 + c], ALU.mult, ALU.add, [rk], [("yc", c)]))
                    else:
                        ops.append(lambda acc=acc, sj=srcs[j], wj=wj, c=c, rk=rk: self.stt(
                            acc, sj, wj, acc, ALU.mult, ALU.add, [rk, ("yc", c)], [("yc", c)]))
        nt = 31
        groups = [ops[i:i + nt] for i in range(0, len(ops), nt)]
        return pe_ops + [grp[j] for j in range(nt) for grp in groups]

    def drain(self, n=None):
        q = self.deferred
        k = len(q) if n is None else min(n, len(q))
        for _ in range(k):
            q.pop(0)()

    def conv(self, g, l, p_full):
        pb = l * PV_L
        self.drain()
        for (a, b) in p_full:
            n = b - a
            bmu, bmk = self.pj()
            for c in range(8):
                self.mm(bmu[:, 0:n], self.ones_l[:], self.yc[:, c, a:b], c == 0, c == 7, [("yc", c)], [bmk])
            bvar, bvk = self.pj()
            for c in range(8):
                self.tt(self.yc[:, c, a:b], self.yc[:, c, a:b], bmu[:, 0:n], ALU.subtract, [("yc", c), bmk], [("yc", c)])
                t, tk = self.tmp()
                self.act_copy(t[:, 0:n], self.yc[:, c, a:b], [("yc", c)], [tk], func=AF.Square)
                self.mm(bvar[:, 0:n], self.ones_l[:], t[:, 0:n], c == 0, c == 7, [tk], [bvk], sig=True)
            self.rsqrt_inplace(self.rstd[:, a:b], "rstd", bvar[:, 0:n], bvk)
            for c in range(8):
                t, tk = self.tmp()
                self.tt(t[:, 0:n], self.yc[:, c, a:b], self.rstd[:, a:b], ALU.mult, [("yc", c), "rstd"], [tk])
                self.act_copy(self.qa[:, c, a:b], t[:, 0:n], [tk], [("qa", c)], func=AF.Silu,
                              scale=self.pvec[:, pb + 56 + c:pb + 57 + c], bias=self.pvec[:, pb + 64 + c:pb + 65 + c])

    def final(self, g):
        if g == 0:
            pcs, outs = [(512, 544)], [(512, 32, 1024)]
        else:
            pcs, outs = [(0, 512)], [(128 * s, 128, (g - 1) * 512 + 128 * s) for s in range(4)]
        fkeys = [("yc", i) for i in range(4)]
        for (a, b) in pcs:
            n = b - a
            bank, bk = self.pj()
            for c in range(NCH):
                t, tk = self.tmp()
                self.act_copy(t[:, 0:n], self.xT[:, c, a:b], [("x", c)], [tk], func=AF.Square)
                self.mm(bank[:, 0:n], self.ones_r[:], t[:, 0:n], c == 0, c == NCH - 1, [tk], [bk], sig=True)
            self.rsqrt_inplace(self.rstd[:, a:b], "rstd", bank[:, 0:n], bk)
        for (c0, n, row0) in outs:
            for c in range(NCH):
                self.stt(self.fin[:, c, 0:n], self.xT[:, c, c0:c0 + n], self.pvec[:, PV_FIN + c:PV_FIN + c + 1], self.rstd[:, c0:c0 + n],
                         ALU.mult, ALU.mult, [("x", c), "rstd"] + fkeys, fkeys)
            for c4 in range(4):
                bank, bk = self.pj()
                bv = bank[:].rearrange("p (j t) -> p j t", j=4)
                for j in range(4):
                    self.tr(bv[0:n, j, :], self.fin[:, c4 * 4 + j, 0:n], self.identf[:], fkeys, [bk], j == 3)
                self.act_copy(self.xs[0:n, c4 * 512:(c4 + 1) * 512], bank[0:n, :], [bk], ["xs"])
            self.dma_out(self.y[row0:row0 + n, :], self.xs[0:n, :], ["xs"])


_W_IN_ORDER = None


def _win_cols():
    chunks = []
    for c in range(8):
        chunks.append(np.arange(128 * c, 128 * c + 128))
    for g in range(4):
        cols = np.arange(1024 + 64 * g, 1024 + 64 * g + 64)
        chunks.append(np.concatenate([cols, cols]))
    for c in range(2):
        chunks.append(np.arange(1280 + 128 * c, 1280 + 128 * c + 128))
    for c in range(8):
        chunks.append(np.arange(1536 + 128 * c, 1536 + 128 * c + 128))
        chunks.append(np.arange(2560 + 128 * c, 2560 + 128 * c + 128))
    for c in range(16):
        chunks.append(np.arange(3584 + 128 * c, 3584 + 128 * c + 128))
        chunks.append(np.arange(5632 + 128 * c, 5632 + 128 * c + 128))
    return np.concatenate(chunks)


def _blockify(w, ncolblk=128):
    L, K, N = w.shape
    nk = K // 128
    nb = N // ncolblk
    r = w.reshape(L, nk, 128, nb, ncolblk).transpose(0, 3, 2, 1, 4)
    return np.ascontiguousarray(r).reshape(L, nb, 128, nk * ncolblk)


def _rope_tables(pos):
    half = 8
    inv = (np.float32(500000.0) ** (-np.arange(half, dtype=np.float32) * np.float32(2.0) / np.float32(16.0))).astype(np.float32)
    ang = pos.astype(np.float32)[None, :] * inv[:, None]
    cos = np.cos(ang).astype(np.float32)
    sin = np.sin(ang).astype(np.float32)
    T = pos.shape[0]
    C = np.ones((64, T), np.float32)
    Sn = np.zeros((64, T), np.float32)
    C[0:8] = cos
    C[8:16] = cos
    Sn[0:8] = sin
    Sn[8:16] = sin
    return np.concatenate([C, C], 0), np.concatenate([Sn, Sn], 0)


_PROG_CACHE = {}


RUN_DEPTH = DEPTH
RUN_CORES = list(range(8))
STOP = None
DBG = None


def _get_prog():
    if "nc" not in _PROG_CACHE:
        _PROG_CACHE["nc"] = Prog(depth=RUN_DEPTH).build()
    return _PROG_CACHE["nc"]


def kernel(x_prompt, x_sample, cache_k, cache_v, state_conv, meta_tokens, w_in, attn_sinks, w_attn_out,
           conv_dw, conv_dw_bias, conv_ln_g, conv_ln_b, w_conv_out, b_conv_out, w_o, norm_mix, norm_ffn,
           w_ffn_in, w_ffn_out, norm_final):
    f = lambda a: np.asarray(a, dtype=np.float32)
    x_prompt, x_sample, cache_k, cache_v, state_conv, meta_tokens = map(f, (x_prompt, x_sample, cache_k, cache_v, state_conv, meta_tokens))
    w_in, w_attn_out, w_conv_out, w_o, w_ffn_in, w_ffn_out = map(f, (w_in, w_attn_out, w_conv_out, w_o, w_ffn_in, w_ffn_out))
    L = DEPTH
    LW = RUN_DEPTH
    w_in, w_attn_out, w_conv_out, w_o, w_ffn_in, w_ffn_out = (a[:LW] for a in (w_in, w_attn_out, w_conv_out, w_o, w_ffn_in, w_ffn_out))
    win_r = _blockify(w_in[:, :, _win_cols()])
    wao_r = _blockify(w_attn_out)
    wco_r = _blockify(w_conv_out)
    wo_r = _blockify(w_o)
    f1cols = np.concatenate([np.concatenate([np.arange(128 * j, 128 * j + 128), np.arange(5632 + 128 * j, 5632 + 128 * j + 128)])
                             for j in range(44)])
    wf1_r = _blockify(w_ffn_in[:, :, f1cols])
    wf2_r = np.ascontiguousarray(w_ffn_out.reshape(LW, 4, 11, 128, 16, 128).transpose(0, 1, 4, 3, 2, 5)).reshape(LW, 4, 16, 128, 11 * 128)
    pvec = np.zeros((128, NV), np.float32)
    fm = lambda v, n: np.asarray(v, np.float32).reshape(n, 128).T
    for l in range(L):
        b = l * PV_L
        pvec[:, b:b + 16] = fm(norm_mix[l], 16)
        pvec[:, b + 16:b + 32] = fm(norm_ffn[l], 16)
        pvec[:, b + 32:b + 48] = fm(b_conv_out[l], 16)
        pvec[:, b + 48:b + 56] = fm(conv_dw_bias[l], 8)
        pvec[:, b + 56:b + 64] = fm(conv_ln_g[l], 8)
        pvec[:, b + 64:b + 72] = fm(conv_ln_b[l], 8)
        dw = np.asarray(conv_dw[l], np.float32)
        pvec[:, b + 72:b + 320] = dw.reshape(31, 8, 128).transpose(2, 1, 0).reshape(128, 248)
        pvec[:, PV_SINK + 16 * l:PV_SINK + 16 * l + 16] = np.asarray(attn_sinks[l], np.float32)[None, :]
    pvec[:, PV_FIN:PV_FIN + 16] = fm(norm_final, 16)
    identf = np.eye(128, dtype=np.float32)
    rmat = np.zeros((128, 128), np.float32)
    for base in (0, 64):
        for d in range(8):
            rmat[base + d + 8, base + d] = -1.0
            rmat[base + d, base + d + 8] = 1.0
    ii = np.arange(128)[:, None]
    jj = np.arange(128)[None, :]
    m_prev = np.where(jj >= ii, 0.0, NEG).astype(np.float32)
    m_own = np.where(jj <= ii, 0.0, NEG).astype(np.float32)
    m_std = np.concatenate([m_prev, m_own], 1)
    colv = (jj >= 112)
    m3 = np.concatenate([np.full((128, 128), NEG, np.float32), np.where((jj <= ii) & colv, 0.0, NEG).astype(np.float32)], 1)
    m4 = np.concatenate([np.where((jj >= ii) & colv, 0.0, NEG).astype(np.float32), m_own], 1)
    spos = 16384 + np.tile(np.arange(TS), NSEQ_S)
    shared = dict(win_r=win_r, wao_r=wao_r, wco_r=wco_r, wo_r=wo_r, wf1_r=wf1_r, wf2_r=wf2_r, pvec=pvec, identf=identf, rmat=rmat)
    in_maps = []
    for core in RUN_CORES:
        sq, ci = core // 4, core % 4
        xin = np.zeros((12 * 128 + 32, D), np.float32)
        pos = np.zeros((12 * 128,), np.int64)
        valid = np.ones((128, 512), np.float32)
        for s in range(12):
            tile = ci * 8 + (s - 4)
            if tile >= 0:
                xin[128 * s:128 * s + 128] = x_prompt[sq, 128 * tile:128 * tile + 128]
                pos[128 * s:128 * s + 128] = 16 + 128 * tile + np.arange(128)
            elif tile == -1:
                xin[128 * s + 112:128 * s + 128] = meta_tokens
                pos[128 * s + 112:128 * s + 128] = np.arange(16)
                valid[:, 128 * s:128 * s + 112] = 0.0
            else:
                valid[:, 128 * s:128 * s + 128] = 0.0
        xin[1536:1568] = x_sample[4 * core:4 * core + 4].reshape(32, D)
        rope = np.zeros((3, 2, 128, 544), np.float32)
        for g in range(3):
            p = pos[512 * g:512 * g + 512]
            p = np.concatenate([p, spos if g == 0 else np.zeros(32, np.int64)])
            rope[g, 0], rope[g, 1] = _rope_tables(p)
        masks = np.stack([m_std, m3 if ci == 0 else m_std, m4 if ci == 0 else m_std]).astype(np.float32)
        m = dict(shared)
        m.update(xin=xin, rope=rope, masks=masks, valid=valid,
                 ck=np.ascontiguousarray(cache_k[:, 4 * core:4 * core + 4].reshape(L, 4, 128, 256)),
                 cv=np.ascontiguousarray(cache_v[:, 4 * core:4 * core + 4].reshape(L, 4, 128, 256)),
                 sc=np.ascontiguousarray(state_conv[:, 4 * core:4 * core + 4]))
        in_maps.append(m)
    nc = _get_prog()
    res = run_bass_kernel_spmd(nc, in_maps, core_ids=list(range(len(RUN_CORES)))).results
    y_prompt = np.zeros((2, SEQ, D), np.float32)
    y_sample = np.zeros((32, TS, D), np.float32)
    nk_p = np.zeros((L, 2, 128, 4, 64), np.float32)
    nv_p = np.zeros((L, 2, 128, 4, 64), np.float32)
    nc_p = np.zeros((L, 2, 30, 1024), np.float32)
    nk_s = np.zeros((L, 32, 128, 4, 64), np.float32)
    nv_s = np.zeros((L, 32, 128, 4, 64), np.float32)
    nc_s = np.zeros((L, 32, 30, 1024), np.float32)
    for ri, core in enumerate(RUN_CORES):
        sq, ci = core // 4, core % 4
        r = res[ri]
        y_prompt[sq, 1024 * ci:1024 * ci + 1024] = r["y"][0:1024]
        y_sample[4 * core:4 * core + 4] = r["y"][1024:1056].reshape(4, TS, D)
        nk_s[:, 4 * core:4 * core + 4] = r["kso"].reshape(L, 4, 128, 4, 64)
        nv_s[:, 4 * core:4 * core + 4] = r["vso"].reshape(L, 4, 128, 4, 64)
        nc_s[:, 4 * core:4 * core + 4] = r["cso"]
        if ci == 3:
            nk_p[:, sq] = r["klast"].reshape(L, 128, 4, 64)
            nv_p[:, sq] = r["vlast"].reshape(L, 128, 4, 64)
            nc_p[:, sq] = r["clast"]
    return (y_prompt, y_sample, nk_p, nv_p, nc_p, nk_s, nv_s, nc_s)
```

```python
import contextlib
import numpy as np
import concourse.bass as bass
import concourse.mybir as mybir
from concourse.bass_utils import run_bass_kernel_spmd

F32 = mybir.dt.float32
BF16 = mybir.dt.bfloat16
ALU = mybir.AluOpType
AF = mybir.ActivationFunctionType
AX = mybir.AxisListType

D = 2048
NCH = 16
DEPTH = 4
SEQ = 4096
NSEQ_S = 4
TS = 8
NEG = -30000.0
EPS = 1e-6
SCALE = 0.125
PV_L = 320
PV_FIN = DEPTH * PV_L
PV_SINK = PV_FIN + 16
NV = PV_SINK + DEPTH * 16
NWSLOT = 5
NBLK_L = 262
USE_SCRATCH = False
NLANE = 3
NTMP = 4


class Tok:
    __slots__ = ("sem", "val", "eng")

    def __init__(self, eng):
        self.sem = None
        self.val = None
        self.eng = eng


class Sched:
    ENGS = ("pe", "act", "dve", "pool", "sp")
    EPOCH = 20000

    def __init__(self, nc):
        self.nc = nc
        self.ops = {e: [] for e in self.ENGS}
        self.cnt = {}
        self.last_w = {}
        self.readers = {}
        self.pending = {e: [] for e in self.ENGS}
        self.nsig = {e: 0 for e in self.ENGS}
        self.last_dma = {}
        self.out_toks = []

    @staticmethod
    def _is_psum(k):
        return (isinstance(k, tuple) and k[0] == "pj") or k in ("pS", "pTO")

    def op(self, eng, fn, reads=(), writes=(), dma=None, signal=True, extra=()):
        writes = list(writes) + [r for r in reads if self._is_psum(r) and r not in writes]
        deps = []
        for r in reads:
            t = self.last_w.get(r)
            if t is not None:
                deps.append(t)
        for w in writes:
            t = self.last_w.get(w)
            if t is not None:
                deps.append(t)
            deps.extend(self.readers.get(w, ()))
        deps.extend(extra)
        tok = Tok(eng if dma is None else "dma")
        if dma is not None:
            t = self.last_dma.get(dma)
            if t is not None:
                deps.append(t)
            tok.sem = "dma_" + dma
            self.cnt[tok.sem] = self.cnt.get(tok.sem, 0) + 16
            tok.val = self.cnt[tok.sem]
            self.last_dma[dma] = tok
            inc = (tok.sem, 16)
        elif signal:
            ep = self.nsig[eng] // self.EPOCH
            self.nsig[eng] += 1
            tok.sem = "%s_%d" % (eng, ep)
            self.cnt[tok.sem] = self.cnt.get(tok.sem, 0) + 1
            tok.val = self.cnt[tok.sem]
            for p in self.pending[eng]:
                p.sem, p.val = tok.sem, tok.val
            self.pending[eng] = []
            inc = (tok.sem, 1)
        else:
            self.pending[eng].append(tok)
            inc = None
        waits = []
        seen = set()
        for t in deps:
            if id(t) in seen:
                continue
            seen.add(id(t))
            if eng in ("pe", "act") and t.eng == eng:
                continue
            waits.append(t)
        self.ops[eng].append((fn, waits, inc))
        for r in reads:
            self.readers.setdefault(r, []).append(tok)
        for w in writes:
            self.last_w[w] = tok
            self.readers[w] = []
        return tok

    def build(self):
        nc = self.nc
        for e in self.ENGS:
            assert not self.pending[e], "unsignalled tail ops on " + e
        sems = {name: nc.alloc_semaphore(name) for name in sorted(self.cnt)}
        eng_attr = {"pe": "tensor", "act": "scalar", "dve": "vector", "pool": "gpsimd", "sp": "sync"}
        with nc.Block() as block:
            for e in self.ENGS:
                ops = self.ops[e]

                def body(engine, ops=ops):
                    waited = {}
                    for fn, waits, inc in ops:
                        for t in waits:
                            assert t.sem is not None
                            if waited.get(t.sem, 0) >= t.val:
                                continue
                            waited[t.sem] = t.val
                            engine.wait_ge(sems[t.sem], t.val)
                        if fn is None:
                            continue
                        ins = fn(engine)
                        if inc is not None:
                            ins.then_inc(sems[inc[0]], inc[1])

                getattr(block, eng_attr[e])(body)


class Prog:
    def __init__(self, depth=DEPTH, ngroups=3):
        self.depth = depth
        self.ngroups = ngroups
        self.nc = bass.Bass("TRN2", target_bir_lowering=False)
        self.S = Sched(self.nc)
        self.es = contextlib.ExitStack()
        self.rot = {}

    def dram(self, name, shape, kind):
        return self.nc.dram_tensor(name, list(shape), F32, kind=kind).ap()

    def sb(self, name, shape, dt):
        return self.es.enter_context(self.nc.sbuf_tensor("s_" + name, list(shape), dt))

    def ps(self, name, shape, dt):
        return self.es.enter_context(self.nc.psum_tensor("p_" + name, list(shape), dt))

    def nxt(self, what, n):
        i = self.rot.get(what, 0)
        self.rot[what] = i + 1
        return i % n

    def pj(self):
        i = self.nxt("pj", 6)
        return self.pjb[i], ("pj", i)

    def tmp(self):
        i = self.nxt("tmp", NTMP)
        return self.tmps[i], ("t", i)

    def PE(self, fn, r=(), w=(), sig=True):
        return self.S.op("pe", fn, r, w, signal=sig)

    def ACT(self, fn, r=(), w=()):
        return self.S.op("act", fn, r, w)

    def DVE(self, fn, r=(), w=()):
        return self.S.op("dve", fn, r, w)

    def act_copy(self, out, in_, r, w, func=AF.Copy, **kw):
        return self.ACT(lambda e: e.activation(out=out, in_=in_, func=func, **kw), r, w)

    def dve_copy(self, out, in_, r, w):
        return self.DVE(lambda e: e.tensor_copy(out=out, in_=in_), r, w)

    def tt(self, out, in0, in1, op, r, w):
        return self.DVE(lambda e: e.tensor_tensor(out=out, in0=in0, in1=in1, op=op), r, w)

    def ts(self, out, in0, s1, s2, op0, op1, r, w):
        if s2 is None:
            return self.DVE(lambda e: e.tensor_scalar(out=out, in0=in0, scalar1=s1, scalar2=None, op0=op0), r, w)
        return self.DVE(lambda e: e.tensor_scalar(out=out, in0=in0, scalar1=s1, scalar2=s2, op0=op0, op1=op1), r, w)

    def stt(self, out, in0, sc, in1, op0, op1, r, w):
        return self.DVE(lambda e: e.scalar_tensor_tensor(out=out, in0=in0, scalar=sc, in1=in1, op0=op0, op1=op1), r, w)

    def mm(self, out, lhsT, rhs, start, stop, r, w, sig=None):
        if DBG == "dmaonly":
            return None
        return self.PE(lambda e: e.matmul(out, lhsT=lhsT, rhs=rhs, start=start, stop=stop), r, w,
                       sig=(stop if sig is None else sig))

    def tr(self, out, in_, ident, r, w, sig):
        return self.PE(lambda e: e.transpose(out=out, in_=in_, identity=ident), r, w, sig=sig)

    def dma_in(self, out, in_, w, key, r=()):
        return self.S.op("sp", lambda e: e.dma_start(out=out, in_=in_), r, w, dma=key)

    def dma_out(self, out, in_, r):
        key = "out%d" % self.nxt("outk", 8)
        t = self.S.op("sp", lambda e: e.dma_start(out=out, in_=in_), r, (), dma=key)
        self.S.out_toks.append(t)
        return t

    def load_w(self, src, nk, ncols):
        i = self.nxt("w", NWSLOT)
        slot = self.wsl[i]
        n = nk * ncols
        dst = slot[:, 0:n]
        bid = self.bctr
        self.bctr += 1
        sc = self.wscs[bid // NBLK_L][bid % NBLK_L, :, 0:n]
        if self.cur_g == 0 or not USE_SCRATCH:
            self.S.op("pool", lambda e: e.dma_start(out=dst, in_=src), (), [("w", i)], dma="w%d" % i)
            if USE_SCRATCH and self.ngroups > 1:
                self.S.op("sp", lambda e: e.dma_start(out=sc, in_=dst), [("w", i)], [("wsc", bid)], dma="wb%d" % i)
        else:
            self.S.op("pool", lambda e: e.dma_start(out=dst, in_=sc), [("wsc", bid)], [("w", i)], dma="w%d" % i)
        return dst.rearrange("p (k n) -> p k n", k=nk), ("w", i)

    def rsqrt_inplace(self, ap, key, src, srckey):
        self.ts(ap, src, EPS, None, ALU.add, None, [srckey], [key])
        self.act_copy(ap, ap, [key], [key], func=AF.Sqrt)
        self.DVE(lambda e: e.reciprocal(out=ap, in_=ap), [key], [key])

    def build(self):
        with self.es:
            self._alloc()
            self._consts()
            for g in range(self.ngroups):
                if STOP == "consts":
                    break
                self.group(g)
            self.S.op("sp", None, extra=list(self.S.out_toks), signal=False)
            self.S.pending["sp"] = []
            self.S.build()
        return self.nc

    def _alloc(self):
        dp = self.depth
        I, O = "ExternalInput", "ExternalOutput"
        self.xin = self.dram("xin", [12 * 128 + 32, D], I)
        self.ck = self.dram("ck", [DEPTH, NSEQ_S, 128, 256], I)
        self.cv = self.dram("cv", [DEPTH, NSEQ_S, 128, 256], I)
        self.sc = self.dram("sc", [DEPTH, NSEQ_S, 30, 1024], I)
        self.pvec_d = self.dram("pvec", [128, NV], I)
        self.rope_d = self.dram("rope", [3, 2, 128, 544], I)
        self.masks_d = self.dram("masks", [3, 128, 256], I)
        self.valid_d = self.dram("valid", [128, 512], I)
        self.ident_d = self.dram("identf", [128, 128], I)
        self.rm_d = self.dram("rmat", [128, 128], I)
        self.win = self.dram("win_r", [dp, 62, 128, 16 * 128], I)
        self.wao = self.dram("wao_r", [dp, 16, 128, 8 * 128], I)
        self.wco = self.dram("wco_r", [dp, 16, 128, 8 * 128], I)
        self.wo = self.dram("wo_r", [dp, 16, 128, 16 * 128], I)
        self.wf1 = self.dram("wf1_r", [dp, 88, 128, 16 * 128], I)
        self.wf2 = self.dram("wf2_r", [dp, 4, 16, 128, 11 * 128], I)
        self.wscs = [self.nc.dram_tensor("wsc%d" % l, [NBLK_L, 128, 2048], BF16, kind="Internal").ap() for l in range(dp)]
        self.y = self.dram("y", [1024 + 32, D], O)
        self.klast = self.dram("klast", [DEPTH, 128, 256], O)
        self.vlast = self.dram("vlast", [DEPTH, 128, 256], O)
        self.clast = self.dram("clast", [DEPTH, 30, 1024], O)
        self.kso = self.dram("kso", [DEPTH, NSEQ_S, 128, 256], O)
        self.vso = self.dram("vso", [DEPTH, NSEQ_S, 128, 256], O)
        self.cso = self.dram("cso", [DEPTH, NSEQ_S, 30, 1024], O)

        sb = self.sb
        T = 544
        self.xT = sb("xT", [128, 16, T], F32)
        self.hT = sb("hT", [128, 16, T], BF16)
        self.qa = sb("qa", [128, 8, T], BF16)
        self.oT = sb("oT", [128, 8, T], BF16)
        self.kbl = sb("kbl", [128, 4, 128 + T], BF16)
        self.kbh = sb("kbh", [128, 4, 128 + T], BF16)
        self.vb = sb("vb", [128, 2, 128 + T], BF16)
        self.kf = sb("kf", [128, 4, 160], F32)
        self.vf = sb("vf", [128, 2, 160], F32)
        self.u = sb("u", [128, 8, T], BF16)
        self.uo = sb("uo", [128, 8, 32], F32)
        self.dg = [sb("dg%d" % i, [128, 128], BF16) for i in range(6)]
        self.us = sb("us", [128, 8, NSEQ_S, 38], F32)
        self.yc = sb("yc", [128, 8, T], F32)
        ycf = self.yc[:].rearrange("p c t -> p (c t)")
        self.mg = ycf.bitcast(BF16).rearrange("p (c t) -> p c t", c=16)
        self.fa = ycf.bitcast(BF16).rearrange("p (c t) -> p c t", c=16)
        self.fin = ycf[:, 0:2048].rearrange("p (c t) -> p c t", c=16)
        self.rstd = sb("rstd", [128, T], F32)
        self.tmps = [sb("tmp%d" % i, [128, 512], F32) for i in range(NTMP)]
        self.tbs = [sb("tb%d" % i, [128, 512], BF16) for i in range(1)]
        self.wsl = [sb("wsl%d" % i, [128, 2048], BF16) for i in range(NWSLOT)]
        self.vtl = [sb("vtl%d" % i, [128, 4, 128], BF16) for i in range(3)]
        self.vth = [sb("vth%d" % i, [128, 4, 128], BF16) for i in range(3)]
        self.kcl = sb("kcl", [128, 4, 128], BF16)
        self.kch = sb("kch", [128, 4, 128], BF16)
        self.vcl = sb("vcl", [128, 4, 128], BF16)
        self.vch = sb("vch", [128, 4, 128], BF16)
        self.vol = sb("vol", [128, 4, 128], BF16)
        self.voh = sb("voh", [128, 4, 128], BF16)
        self.pn = [sb("pn%d" % i, [128, 2, 256], BF16) for i in range(NLANE)]
        self.ptsb = [sb("ptsb%d" % i, [128, 4, 128], BF16) for i in range(NLANE)]
        self.st = [sb("st%d" % i, [128, 16], F32) for i in range(NLANE)]
        self.masks = sb("masks", [128, 3, 256], F32)
        self.maskb = sb("maskb", [128, 3, 256], BF16)
        self.Ct = sb("Ct", [128, T], F32)
        self.St = sb("St", [128, T], F32)
        self.valid = sb("valid", [128, 512], F32)
        self.pvec = sb("pvecs", [128, NV], F32)
        self.identf = sb("identfs", [128, 128], F32)
        self.identb = sb("identb", [128, 128], BF16)
        self.rmf = sb("rmf", [128, 128], F32)
        self.rmb = sb("rmb", [128, 128], BF16)
        self.ones_r = sb("ones_r", [128, 128], F32)
        self.ones_l = sb("ones_l", [128, 128], F32)
        self.cK = [sb("cK%d" % l, [128, 4, 128], BF16) for l in range(dp)]
        self.cV = [sb("cV%d" % l, [128, 2, 128], BF16) for l in range(dp)]
        self.cU = [sb("cU%d" % l, [128, 8, 30], BF16) for l in range(dp)]
        self.xs = sb("xs", [128, D], F32)
        self.pjb = [self.ps("pj%d" % i, [128, 512], F32) for i in range(6)]
        self.pS = self.ps("pS", [128, 2, 256], F32)
        pto = self.ps("pTO", [128, 512], F32)
        self.pO = pto[:, 0:256]
        self.pT = pto[:, 256:512].bitcast(BF16).rearrange("p (j t) -> p j t", j=4)
        self.laneS = [(self.pS, "pS")]
        self.laneT = [(self.pT, "pTO")]
        self.laneO = [(self.pO, "pTO")]
        for ln in range(1, NLANE):
            bs, bt = 6 - 2 * ln, 7 - 2 * ln
            self.laneS.append((self.pjb[bs][:].rearrange("p (h k) -> p h k", h=2), ("pj", bs)))
            self.laneT.append((self.pjb[bt][:, 256:512].bitcast(BF16).rearrange("p (j t) -> p j t", j=4), ("pj", bt)))
            self.laneO.append((self.pjb[bt][:, 0:256], ("pj", bt)))

    def _consts(self):
        S = self.S
        toks = []
        toks.append(self.dma_in(self.pvec[:], self.pvec_d, ["pvec"], "c0"))
        toks.append(self.dma_in(self.masks[:], self.masks_d.rearrange("m p k -> p m k"), ["masks"], "c1"))
        toks.append(self.dma_in(self.valid[:], self.valid_d, ["valid"], "c2"))
        toks.append(self.dma_in(self.identf[:], self.ident_d, ["identf"], "c3"))
        toks.append(self.dma_in(self.rmf[:], self.rm_d, ["rmf"], "c4"))
        self.dve_copy(self.identb[:], self.identf[:], ["identf"], ["identb"])
        self.dve_copy(self.maskb[:], self.masks[:], ["masks"], ["maskb"])
        self.dve_copy(self.rmb[:], self.rmf[:], ["rmf"], ["rmb"])
        self.DVE(lambda e: e.memset(self.ones_r[:], 1.0 / 2048.0), (), ["ones_r"])
        self.DVE(lambda e: e.memset(self.ones_l[:], 1.0 / 1024.0), (), ["ones_l"])
        for i in range(3):
            self.DVE(lambda e, i=i: e.memset(self.vtl[i][:], 0.0), (), [("vt", i)])
            self.DVE(lambda e, i=i: e.memset(self.vth[i][:], 0.0), (), [("vt", i)])
        for t_, k in ((self.vcl, "vc"), (self.vch, "vc"), (self.vol, "vo"), (self.voh, "vo")):
            self.DVE(lambda e, t_=t_: e.memset(t_[:], 0.0), (), [k])
        self.DVE(lambda e: e.memset(self.u[:], 0.0), (), [("u", c) for c in range(8)])
        self.DVE(lambda e: e.memset(self.kbl[:], 0.0), (), [("kb", g) for g in range(4)])
        self.DVE(lambda e: e.memset(self.kbh[:], 0.0), (), [("kb", g) for g in range(4)])
        self.DVE(lambda e: e.memset(self.kcl[:], 0.0), (), ["kc"])
        self.DVE(lambda e: e.memset(self.kch[:], 0.0), (), ["kc"])
        self.DVE(lambda e: e.memset(self.vb[:], 0.0), (), [("vb", c) for c in range(2)])
        last = self.DVE(lambda e: e.memset(self.rstd[:], 0.0), (), ["rstd"])
        for eng in ("pe", "act", "dve", "pool", "sp"):
            S.op(eng, None, extra=toks + [last], signal=False)
            S.pending[eng] = []

    def pieces(self, g, c0):
        out = []
        if c0 < 512:
            out.append((c0, 512))
        if g == 0:
            out.append((512, 544))
        return out

    def group(self, g):
        self.dma_in(self.Ct[:], self.rope_d[g, 0], ["Ct"], "rope")
        self.dma_in(self.St[:], self.rope_d[g, 1], ["St"], "rope2")
        loads = [(4 * g + s, 128, 128 * s) for s in range(4)]
        if g == 0:
            loads.append((12, 32, 512))
        for (slot, n, c0) in loads:
            self.dma_in(self.xs[0:n, :], self.xin[slot * 128:slot * 128 + n, :], ["xs"], "xin")
            for c4 in range(4):
                bank, bk = self.pj()
                bv = bank[:].rearrange("p (j t) -> p j t", j=4)
                for j in range(4):
                    c = c4 * 4 + j
                    self.tr(bv[:, j, 0:n], self.xs[0:n, c * 128:(c + 1) * 128], self.identf[0:n, 0:n], ["xs"], [bk], j == 3)
                self.act_copy(self.xT[:, c4 * 4:(c4 + 1) * 4, c0:c0 + n], bv[:, :, 0:n], [bk], [("x", c) for c in range(c4 * 4, c4 * 4 + 4)])
        if STOP == "load":
            return
        for l in range(self.depth):
            self.layer(g, l)
        if STOP == "nofinal":
            return
        self.final(g)

    def rmsnorm(self, pcs, gcol, out_fn, okeys):
        for (a, b) in pcs:
            n = b - a
            bank, bk = self.pj()
            for c in range(NCH):
                t, tk = self.tmp()
                self.act_copy(t[:, 0:n], self.xT[:, c, a:b], [("x", c)], [tk], func=AF.Square)
                self.mm(bank[:, 0:n], self.ones_r[:], t[:, 0:n], c == 0, c == NCH - 1, [tk], [bk], sig=True)
            self.rsqrt_inplace(self.rstd[:, a:b], "rstd", bank[:, 0:n], bk)
            for c in range(NCH):
                self.stt(out_fn(c, a, b), self.xT[:, c, a:b], self.pvec[:, gcol + c:gcol + c + 1], self.rstd[:, a:b],
                         ALU.mult, ALU.mult, [("x", c), "rstd"], [okeys(c)])

    def proj(self, wv, wk, nk, rhs_fn, rkeys, a, b):
        n = b - a
        bank, bk = self.pj()
        for k in range(nk):
            self.mm(bank[:, 0:n], wv[:, k, :], rhs_fn(k, a, b), k == 0, k == nk - 1, [wk, rkeys(k)], [bk])
        return bank, bk

    def layer(self, g, l):
        self.bctr = l * NBLK_L
        self.cur_g = g
        pb = l * PV_L
        c_ext = 128 * l if g == 0 else 0
        c_full = 128 * (l + 1) if g == 0 else 0
        p_ext = self.pieces(g, c_ext)
        p_full = self.pieces(g, c_full)
        hrhs = lambda k, a, b: self.hT[:, k, a:b]
        hkey = lambda k: ("h", k)
        if g > 0:
            self.act_copy(self.kbl[0:64, :, 0:128], self.cK[l][0:64], [("cK", l)], [("kb", i) for i in range(4)])
            self.act_copy(self.kbh[64:128, :, 0:128], self.cK[l][64:128], [("cK", l)], [("kb", i) for i in range(4)])
            self.act_copy(self.vb[:, :, 0:128], self.cV[l][:], [("cV", l)], [("vb", i) for i in range(2)])
            self.dve_copy(self.u[:, :, 0:30], self.cU[l][:], [("cU", l)], [("u", c) for c in range(8)])
        else:
            for j in range(NSEQ_S):
                self.dma_in(self.xs[0:30, 0:1024], self.sc[l, j], ["xs"], "cst")
                for hf in range(2):
                    bank, bk = self.pj()
                    bv = bank[:].rearrange("p (j t) -> p j t", j=4)
                    for cc in range(4):
                        c = hf * 4 + cc
                        self.tr(bv[:, cc, 0:30], self.xs[0:30, c * 128:(c + 1) * 128], self.identf[0:30, 0:30], ["xs"], [bk], cc == 3)
                    self.act_copy(self.us[:, hf * 4:(hf + 1) * 4, j, 0:30], bv[:, :, 0:30], [bk], ["us"])
        if STOP == "carry":
            return
        self.rmsnorm(p_ext, pb + 0, lambda c, a, b: self.hT[:, c, a:b], hkey)
        if STOP == "norm1":
            return
        for gi in range(4):
            wv, wk = self.load_w(self.win[l, 8 + gi], 16, 128)
            if DBG == "nomm":
                self.act_copy(self.tbs[0][:, 0:128], wv[:, 0, :], [wk], [("tb", 0)])
                continue
            for (a, b) in p_ext:
                if DBG == "p512" and a >= 512:
                    continue
                if DBG == "p32" and a < 512:
                    continue
                bank, bk = self.proj(wv, wk, 16, hrhs, hkey, a, b)
                dests = [(self.kbl[0:64, gi, 128 + a:128 + b], ("kb", gi), 0, (0, 64)),
                         (self.kbh[64:128, gi, 128 + a:128 + b], ("kb", gi), 0, (64, 128))]
                if b > 384:
                    lo = max(a, 384)
                    dests.append((self.kf[:, gi, lo - 384:b - 384], "kf", lo - a))
                if DBG in ("norot", "p512", "p32", "dvesecond"):
                    self.act_copy(dests[0][0], bank[0:64, 0:b - a], [bk], [dests[0][1]])
                elif DBG == "noproj":
                    pass
                else:
                    self.rot_evac(bank, bk, a, b, dests)
        if STOP == "kproj":
            return
        for c in range(2):
            wv, wk = self.load_w(self.win[l, 12 + c], 16, 128)
            for (a, b) in p_ext:
                n = b - a
                bank, bk = self.proj(wv, wk, 16, hrhs, hkey, a, b)
                self.act_copy(self.vb[:, c, 128 + a:128 + b], bank[:, 0:n], [bk], [("vb", c)])
                if b > 384:
                    lo = max(a, 384)
                    self.dve_copy(self.vf[:, c, lo - 384:b - 384], bank[:, lo - a:n], [bk], ["vf"])
        if STOP == "vproj":
            return
        for c in range(8):
            wv1, wk1 = self.load_w(self.win[l, 14 + 2 * c], 16, 128)
            wv2, wk2 = self.load_w(self.win[l, 15 + 2 * c], 16, 128)
            for (a, b) in p_ext:
                n = b - a
                b1, k1 = self.proj(wv1, wk1, 16, hrhs, hkey, a, b)
                b2, k2 = self.proj(wv2, wk2, 16, hrhs, hkey, a, b)
                t, tk = self.tmp()
                self.act_copy(t[:, 0:n], b2[:, 0:n], [k2], [tk], func=AF.Sigmoid)
                if a < 512:
                    self.tt(self.u[:, c, 30 + a:30 + b], b1[:, 0:n], t[:, 0:n], ALU.mult, [k1, tk], [("u", c)])
                    if g == self.ngroups - 1:
                        self.tt(self.uo[:, c, 0:30], b1[:, 482 - a:512 - a], t[:, 482 - a:512 - a], ALU.mult, [k1, tk], ["uo"])
                else:
                    self.tt(self.us[:, c, :, 30:38], b1[:, 0:32].rearrange("p (s t) -> p s t", s=NSEQ_S),
                            t[:, 0:32].rearrange("p (s t) -> p s t", s=NSEQ_S), ALU.mult, [k1, tk], ["us"])
        if g == 0:
            ukeys = [("u", c) for c in range(8)]
            if c_ext < 512:
                nv_ = 512 - c_ext
                self.tt(self.u[:, :, 30 + c_ext:542], self.u[:, :, 30 + c_ext:542],
                        self.valid[:, c_ext:512].unsqueeze(1).to_broadcast([128, 8, nv_]), ALU.mult, ukeys, ukeys)
        if STOP == "glu":
            return
        for c in range(8):
            wv, wk = self.load_w(self.win[l, c], 16, 128)
            for (a, b) in p_full:
                bank, bk = self.proj(wv, wk, 16, hrhs, hkey, a, b)
                self.rot_evac(bank, bk, a, b, [(self.qa[:, c, a:b], ("qa", c))])
        if STOP == "qproj":
            return
        if g < self.ngroups - 1:
            self.act_copy(self.cK[l][0:64], self.kbl[0:64, :, 512:640], [("kb", i) for i in range(4)], [("cK", l)])
            self.act_copy(self.cK[l][64:128], self.kbh[64:128, :, 512:640], [("kb", i) for i in range(4)], [("cK", l)])
            self.act_copy(self.cV[l][:], self.vb[:, :, 512:640], [("vb", i) for i in range(2)], [("cV", l)])
            self.dve_copy(self.cU[l][:], self.u[:, :, 512:542], [("u", c) for c in range(8)], [("cU", l)])
        if g == self.ngroups - 1:
            self.out_rows(l, 128, 0, 482, self.klast[l], self.vlast[l], self.clast[l])
        if g == 0:
            self.out_rows(l, 32, 128, None, None, None, None)
        if STOP == "inproj":
            return
        self.deferred = self.conv_taps(g, l, p_full)
        n_units = 8 * max(0, 4 - c_full // 128) + (8 * NSEQ_S if g == 0 else 0)
        self.per_unit = -(-len(self.deferred) // max(1, n_units))
        self.attention(g, l, c_full)
        if STOP == "attn":
            return
        self.conv(g, l, p_full)
        for c in range(NCH):
            wa, ka = self.load_w(self.wao[l, c], 8, 128)
            wc, kc_ = self.load_w(self.wco[l, c], 8, 128)
            wga, kga = self.load_w(self.win[l, 30 + 2 * c], 16, 128)
            wgb, kgb = self.load_w(self.win[l, 31 + 2 * c], 16, 128)
            for (a, b) in p_full:
                n = b - a
                pa, pak = self.proj(wa, ka, 8, lambda k, a, b: self.oT[:, k, a:b], lambda k: ("o", k), a, b)
                pga, pgak = self.proj(wga, kga, 16, hrhs, hkey, a, b)
                t1, t1k = self.tmp()
                self.act_copy(t1[:, 0:n], pga[:, 0:n], [pgak], [t1k], func=AF.Sigmoid)
                self.tt(t1[:, 0:n], pa[:, 0:n], t1[:, 0:n], ALU.mult, [pak, t1k], [t1k])
                pc, pck = self.proj(wc, kc_, 8, lambda k, a, b: self.qa[:, k, a:b], lambda k: ("qa", k), a, b)
                pgb, pgbk = self.proj(wgb, kgb, 16, hrhs, hkey, a, b)
                t2, t2k = self.tmp()
                self.act_copy(t2[:, 0:n], pgb[:, 0:n], [pgbk], [t2k], func=AF.Sigmoid)
                self.stt(t2[:, 0:n], pc[:, 0:n], self.pvec[:, pb + 32 + c:pb + 33 + c], t2[:, 0:n], ALU.add, ALU.mult,
                         [pck, t2k], [t2k])
                self.tt(self.mg[:, c, a:b], t1[:, 0:n], t2[:, 0:n], ALU.add, [t1k, t2k], [("yc", c // 2)])
        if STOP == "merge":
            return
        for c in range(NCH):
            wv, wk = self.load_w(self.wo[l, c], 16, 128)
            for (a, b) in p_full:
                n = b - a
                bank, bk = self.proj(wv, wk, 16, lambda k, a, b: self.mg[:, k, a:b], lambda k: ("yc", k // 2), a, b)
                self.tt(self.xT[:, c, a:b], self.xT[:, c, a:b], bank[:, 0:n], ALU.add, [bk, ("x", c)], [("x", c)])
        self.rmsnorm(p_full, pb + 16, lambda c, a, b: self.hT[:, c, a:b], hkey)
        for hg in range(4):
            for jj in range(11):
                j = hg * 11 + jj
                w1, k1 = self.load_w(self.wf1[l, 2 * j], 16, 128)
                w2, k2 = self.load_w(self.wf1[l, 2 * j + 1], 16, 128)
                for (a, b) in p_full:
                    n = b - a
                    b1, bk1 = self.proj(w1, k1, 16, hrhs, hkey, a, b)
                    b2, bk2 = self.proj(w2, k2, 16, hrhs, hkey, a, b)
                    t, tk = self.tmp()
                    self.act_copy(t[:, 0:n], b1[:, 0:n], [bk1], [tk], func=AF.Silu)
                    self.tt(self.fa[:, jj, a:b], t[:, 0:n], b2[:, 0:n], ALU.mult, [tk, bk2], [("yc", jj // 2)])
            for c in range(NCH):
                wv, wk = self.load_w(self.wf2[l, hg, c], 11, 128)
                for (a, b) in p_full:
                    n = b - a
                    bank, bk = self.proj(wv, wk, 11, lambda k, a, b: self.fa[:, k, a:b], lambda k: ("yc", k // 2), a, b)
                    self.tt(self.xT[:, c, a:b], self.xT[:, c, a:b], bank[:, 0:n], ALU.add, [bk, ("x", c)], [("x", c)])

    def rot_evac(self, bank, bk, a, b, dests):
        n = b - a
        i = self.nxt("tb", 1)
        tb, tbk = self.tbs[i], ("tb", i)
        self.act_copy(tb[:, 0:n], bank[:, 0:n], [bk], [tbk])
        t1, t1k = self.tmp()
        self.tt(t1[:, 0:n], bank[:, 0:n], self.Ct[:, a:b], ALU.mult, [bk, "Ct"] + ([tbk] if DBG == "rot1s" else []), [t1k])
        if DBG in ("rot1", "rot1s"):
            for d in dests:
                off = d[2] if len(d) > 2 else 0
                self.act_copy(d[0], t1[:, off:n], [t1k], [d[1]])
            return
        b2, b2k = self.pj()
        self.mm(b2[:, 0:n], self.rmb[:], tb[:, 0:n], True, True, [tbk], [b2k])
        t2, t2k = self.tmp()
        self.tt(t2[:, 0:n], b2[:, 0:n], self.St[:, a:b], ALU.mult, [b2k, "St"], [t2k])
        if DBG == "rot2":
            for d in dests:
                off = d[2] if len(d) > 2 else 0
                self.act_copy(d[0], t2[:, off:n], [t1k, t2k], [d[1]])
            return
        for d in dests:
            off = d[2] if len(d) > 2 else 0
            p0, p1 = d[3] if len(d) > 3 else (0, 128)
            self.tt(d[0], t1[p0:p1, off:n], t2[p0:p1, off:n], ALU.add, [t1k, t2k], [d[1]])

    def out_rows(self, l, n, kcol, ucol, kdst, vdst, cdst):
        bank, bk = self.pj()
        bv = bank[:].rearrange("p (j t) -> p j t", j=4)
        for gi in range(4):
            self.tr(bv[0:n, gi, :], self.kf[:, gi, kcol:kcol + n], self.identf[:], ["kf"], [bk], gi == 3)
        self.act_copy(self.xs[0:n, 0:256].rearrange("p (g d) -> p g d", g=4), bv[0:n, :, 0:64], [bk], ["xs"])
        bank, bk = self.pj()
        bv = bank[:].rearrange("p (j t) -> p j t", j=4)
        for c in range(2):
            self.tr(bv[0:n, c, :], self.vf[:, c, kcol:kcol + n], self.identf[:], ["vf"], [bk], c == 1)
        self.act_copy(self.xs[0:n, 256:512].rearrange("p (g d) -> p g d", g=2), bv[0:n, 0:2, :], [bk], ["xs"])
        m = 30 if n == 128 else 32
        for hf in range(2):
            if n != 128:
                tq, tqk = self.tmp()
                for cc in range(4):
                    self.dve_copy(tq[:, cc * 32:(cc + 1) * 32].rearrange("p (s t) -> p s t", s=NSEQ_S),
                                  self.us[:, hf * 4 + cc, :, 30:38], ["us"], [tqk])
            bank, bk = self.pj()
            bv = bank[:].rearrange("p (j t) -> p j t", j=4)
            for cc in range(4):
                c = hf * 4 + cc
                src = self.uo[:, c, 0:30] if n == 128 else tq[:, cc * 32:(cc + 1) * 32]
                rk = "uo" if n == 128 else tqk
                self.tr(bv[0:m, cc, :], src, self.identf[:], [rk], [bk], cc == 3)
            self.act_copy(self.xs[0:m, 1024 + hf * 512:1536 + hf * 512].rearrange("p (g d) -> p g d", g=4), bv[0:m, :, :], [bk], ["xs"])
        if n == 128:
            self.dma_out(kdst, self.xs[0:128, 0:256], ["xs"])
            self.dma_out(vdst, self.xs[0:128, 256:512], ["xs"])
            self.dma_out(cdst, self.xs[0:30, 1024:2048], ["xs"])
        else:
            self.dma_out(self.kso[l, :, 0:120, :], self.ck[l, :, 8:128, :], [])
            self.dma_out(self.vso[l, :, 0:120, :], self.cv[l, :, 8:128, :], [])
            self.dma_out(self.cso[l, :, 0:22, :], self.sc[l, :, 8:30, :], [])
            for j in range(NSEQ_S):
                self.dma_out(self.kso[l, j, 120:128, :], self.xs[8 * j:8 * j + 8, 0:256], ["xs"])
                self.dma_out(self.vso[l, j, 120:128, :], self.xs[8 * j:8 * j + 8, 256:512], ["xs"])
                self.dma_out(self.cso[l, j, 22:30, :], self.xs[8 * j:8 * j + 8, 1024:2048], ["xs"])

    def build_vt(self, i, col0):
        for c in range(2):
            self.tr(self.pT[:, c, :], self.vb[:, c, col0:col0 + 128], self.identb[:], [("vb", c)], ["pTO"], c == 1)
        src = self.pT[:, 0:2, :].rearrange("p c (h d) -> p (c h) d", h=2)
        self.act_copy(self.vtl[i][:, :, 0:64], src, ["pTO"], [("vt", i)])
        self.dve_copy(self.vth[i][:, :, 64:128], src, ["pTO"], [("vt", i)])

    def attention(self, g, l, c_full):
        s0 = c_full // 128
        prev = None
        for s in range(s0, 4):
            if prev is None:
                prev = self.nxt("vt", 3)
                self.build_vt(prev, 128 * s)
            own = self.nxt("vt", 3)
            self.build_vt(own, 128 + 128 * s)
            plist = []
            for un in range(8):
                kk4 = [("kb", i) for i in range(4)]
                segs = [((self.kbl[:, :, 128 * s:128 * s + 128], self.kbh[:, :, 128 * s:128 * s + 128]), kk4, self.vtl[prev], self.vth[prev], ("vt", prev), 128),
                        ((self.kbl[:, :, 128 + 128 * s:256 + 128 * s], self.kbh[:, :, 128 + 128 * s:256 + 128 * s]), kk4, self.vtl[own], self.vth[own], ("vt", own), 128)]
                mi = 0
                if g == 0 and s == 3:
                    mi = 1
                if g == 1 and s == 0:
                    mi = 2
                plist.append((l, un, 128, 128 * s, segs, mi))
            self.run_units(plist)
            prev = own
        if g == 0 and DBG not in ("vtonly", "nosample", "qk", "exp", "pn", "ptr"):
            for j in range(NSEQ_S):
                stg, sk_ = self.tmp()
                self.dma_in(stg[:, 0:256], self.ck[l, j], [sk_], "stg")
                src = stg[:, 0:256].rearrange("p (g d) -> p g d", g=4)
                stgd_, sdk = self.tmp()
                stgd = stgd_[:].rearrange("p (g d) -> p g d", g=4)
                self.act_copy(stgd[:, :, 0:64], src, [sk_], [sdk])
                self.dve_copy(stgd[:, :, 64:128], src, [sk_], [sdk])
                bank, bk = self.pj()
                bv = bank[:].rearrange("p (j t) -> p j t", j=4)
                for gi in range(4):
                    self.tr(bv[:, gi, :], stgd[:, gi, :], self.identf[:], [sdk], [bk], gi == 3)
                self.act_copy(self.kcl[0:64], bv[0:64], [bk], ["kc"])
                self.act_copy(self.kch[64:128], bv[64:128], [bk], ["kc"])
                stg, sk_ = self.tmp()
                self.dma_in(stg[:, 0:256], self.cv[l, j], [sk_], "stg")
                src = stg[:, 0:256].rearrange("p (g d) -> p g d", g=4)
                self.act_copy(self.vcl[:, :, 0:64], src, [sk_], ["vc"])
                self.dve_copy(self.vch[:, :, 64:128], src, [sk_], ["vc"])
                col = 128 + 512 + TS * j
                for c in range(2):
                    self.tr(self.pT[0:TS, c, :], self.vb[:, c, col:col + TS], self.identb[:], [("vb", c)], ["pTO"], c == 1)
                src = self.pT[0:TS, 0:2, :].rearrange("p c (h d) -> p (c h) d", h=2)
                self.act_copy(self.vol[0:TS, :, 0:64], src, ["pTO"], ["vo"])
                self.dve_copy(self.voh[0:TS, :, 64:128], src, ["pTO"], ["vo"])
                plist = []
                for un in range(8):
                    segs = [((self.kcl[:], self.kch[:]), ["kc"], self.vcl, self.vch, "vc", 128),
                            ((self.kbl[:, :, col:col + TS], self.kbh[:, :, col:col + TS]), [("kb", i) for i in range(4)], self.vol, self.voh, "vo", TS)]
                    plist.append((l, un, TS, 512 + TS * j, segs, 0))
                self.run_units(plist)

    def unit(self, l, un, nq, qc, segs, mi, lane):
        gk = un // 2
        nkt = sum(sg[5] for sg in segs)
        pS, pSk = self.laneS[lane]
        pT, pTk = self.laneT[lane]
        pO, pOk = self.laneO[lane]
        for h in range(2):
            off = 0
            for si, (kap, kkeys, vl, vh, vk, nk) in enumerate(segs):
                self.mm(pS[0:nq, h, off:off + nk], self.identb[0:nq, 0:nq], self.maskb[0:nq, mi, off:off + nk], True, False,
                        [], [pSk], sig=False)
                self.mm(pS[0:nq, h, off:off + nk], self.qa[:, un, qc:qc + nq], kap[h][:, gk, 0:nk], False, True,
                        [("qa", un), kkeys[gk % len(kkeys)]], [pSk], sig=(h == 1 and si == len(segs) - 1))
                off += nk
        yield
        t, tk = self.tmp()
        sm = t[:].rearrange("p (h k) -> p h k", h=2)
        st, stk = self.st[lane], ("st", lane)
        self.DVE(lambda e: e.tensor_reduce(out=st[0:nq, 0:2], in_=pS[0:nq, :, 0:nkt], axis=AX.X, op=ALU.max), [pSk], [stk])
        self.ts(st[0:nq, 2:4], st[0:nq, 0:2], -100.0, -SCALE, ALU.max, ALU.mult, [stk], [stk])
        self.DVE(lambda e: e.memset(st[0:nq, 4:6], 0.0), [stk], [stk])
        yield
        for h in range(2):
            self.ACT(lambda e, h=h: e.activation(out=sm[0:nq, h, 0:nkt], in_=pS[0:nq, h, 0:nkt], func=AF.Exp, bias=st[0:nq, 2 + h:3 + h],
                                                  scale=SCALE, accum_out=st[0:nq, 4 + h:5 + h]), [pSk, stk], [tk, stk])
        yield
        sk = PV_SINK + l * 16 + 2 * un
        self.tt(st[0:nq, 6:8], st[0:nq, 2:4], self.pvec[0:nq, sk:sk + 2], ALU.add, [stk], [stk])
        self.act_copy(st[0:nq, 6:8], st[0:nq, 6:8], [stk], [stk], func=AF.Exp)
        yield
        self.tt(st[0:nq, 8:10], st[0:nq, 4:6], st[0:nq, 6:8], ALU.add, [stk], [stk])
        self.DVE(lambda e: e.reciprocal(out=st[0:nq, 10:12], in_=st[0:nq, 8:10]), [stk], [stk])
        pn, pnk = self.pn[lane], ("pn", lane)
        self.tt(pn[0:nq, :, 0:nkt], sm[0:nq, :, 0:nkt], st[0:nq, 10:12].unsqueeze(2).to_broadcast([nq, 2, nkt]), ALU.mult,
                [tk, stk], [pnk])
        yield
        cnt_ = 0
        for h in range(2):
            off = 0
            for si, sg in enumerate(segs):
                nk = sg[5]
                cnt_ += 1
                self.tr(pT[0:nk, si * 2 + h, 0:nq], pn[0:nq, h, off:off + nk], self.identb[0:nq, 0:nq], [pnk], [pTk],
                        cnt_ == 2 * len(segs))
                off += nk
        ptsb, ptk = self.ptsb[lane], ("ptsb", lane)
        nks = [sg[5] for sg in segs]
        if len(set(nks)) == 1:
            self.act_copy(ptsb[0:nks[0], :, 0:nq], pT[0:nks[0], :, 0:nq], [pTk], [ptk])
        else:
            for si, nk in enumerate(nks):
                self.act_copy(ptsb[0:nk, 2 * si:2 * si + 2, 0:nq], pT[0:nk, 2 * si:2 * si + 2, 0:nq], [pTk], [ptk])
        yield
        idx = 0
        last = 2 * len(segs) - 1
        for h in range(2):
            for si, (kap, kkeys, vl, vh, vk, nk) in enumerate(segs):
                vsel = vl if h == 0 else vh
                self.mm(pO[:, 0:nq], vsel[0:nk, gk, :], ptsb[0:nk, si * 2 + h, 0:nq], idx == 0, idx == last, [vk, ptk], [pOk])
                idx += 1
        self.act_copy(self.oT[:, un, qc:qc + nq], pO[:, 0:nq], [pOk], [("o", un)])
        self.drain(self.per_unit)

    def run_units(self, plist):
        for i in range(0, len(plist), NLANE):
            gens = [self.unit(*plist[i + k], k) for k in range(NLANE) if i + k < len(plist)]
            live = []
            step = 0
            while gens or live:
                if gens and step % 2 == 0:
                    live.append(gens.pop(0))
                for gen in list(live):
                    try:
                        next(gen)
                    except StopIteration:
                        live.remove(gen)
                step += 1

    def conv_taps(self, g, l, p_full):
        pb = l * PV_L
        wcol = pb + 72
        ops = []
        pe_ops = []
        for c in range(8):
            for (a, b) in p_full:
                if a < 512:
                    bias = self.pvec[:, pb + 48 + c:pb + 49 + c]
                    holder = {}
                    for j in range(31):
                        wj = self.pvec[:, wcol + c * 31 + j:wcol + c * 31 + j + 1]

                        def pe_tap(j=j, wj=wj, c=c, a=a, b=b, holder=holder, bias=bias):
                            if j == 0:
                                bi = self.nxt("pjc", 8 - 2 * NLANE)
                                holder["bank"], holder["bk"] = self.pjb[bi], ("pj", bi)
                            bank, bk = holder["bank"], holder["bk"]
                            di = self.nxt("dg", 6)
                            dgt, dgk = self.dg[di], ("dg", di)
                            self.ts(dgt[:], self.identb[:], wj, None, ALU.mult, None, [], [dgk])
                            self.mm(bank[:, 0:b - a], dgt[:], self.u[:, c, a + j:b + j], j == 0, j == 30, [dgk, ("u", c)], [bk], sig=True)
                            if j == 30:
                                self.ts(self.yc[:, c, a:b], bank[:, 0:b - a], bias, None, ALU.add, None, [bk], [("yc", c)])
                        pe_ops.append(pe_tap)
                    continue
                else:
                    acc = self.yc[:, c, 512:544].rearrange("p (s t) -> p s t", s=NSEQ_S)
                    srcs = [self.us[:, c, :, j:j + TS] for j in range(31)]
                    rk = "us"
                for j in range(31):
                    wj = self.pvec[:, wcol + c * 31 + j:wcol + c * 31 + j + 1]
                    if j == 0:
                        ops.append(lambda acc=acc, s0=srcs[0], wj=wj, c=c, rk=rk: self.ts(
                            acc, s0, wj, self.pvec[:, pb + 48 + c:pb + 49 + c], ALU.mult, ALU.add, [rk], [("yc", c)]))
                    else:
                        ops.append(lambda acc=acc, sj=srcs[j], wj=wj, c=c, rk=rk: self.stt(
                            acc, sj, wj, acc, ALU.mult, ALU.add, [rk, ("yc", c)], [("yc", c)]))
        nt = 31
        groups = [ops[i:i + nt] for i in range(0, len(ops), nt)]
        return pe_ops + [grp[j] for j in range(nt) for grp in groups]

    def drain(self, n=None):
        q = self.deferred
        k = len(q) if n is None else min(n, len(q))
        for _ in range(k):
            q.pop(0)()

    def conv(self, g, l, p_full):
        pb = l * PV_L
        self.drain()
        for (a, b) in p_full:
            n = b - a
            bmu, bmk = self.pj()
            for c in range(8):
                self.mm(bmu[:, 0:n], self.ones_l[:], self.yc[:, c, a:b], c == 0, c == 7, [("yc", c)], [bmk])
            bvar, bvk = self.pj()
            for c in range(8):
                self.tt(self.yc[:, c, a:b], self.yc[:, c, a:b], bmu[:, 0:n], ALU.subtract, [("yc", c), bmk], [("yc", c)])
                t, tk = self.tmp()
                self.act_copy(t[:, 0:n], self.yc[:, c, a:b], [("yc", c)], [tk], func=AF.Square)
                self.mm(bvar[:, 0:n], self.ones_l[:], t[:, 0:n], c == 0, c == 7, [tk], [bvk], sig=True)
            self.rsqrt_inplace(self.rstd[:, a:b], "rstd", bvar[:, 0:n], bvk)
            for c in range(8):
                t, tk = self.tmp()
                self.tt(t[:, 0:n], self.yc[:, c, a:b], self.rstd[:, a:b], ALU.mult, [("yc", c), "rstd"], [tk])
                self.act_copy(self.qa[:, c, a:b], t[:, 0:n], [tk], [("qa", c)], func=AF.Silu,
                              scale=self.pvec[:, pb + 56 + c:pb + 57 + c], bias=self.pvec[:, pb + 64 + c:pb + 65 + c])

    def final(self, g):
        if g == 0:
            pcs, outs = [(512, 544)], [(512, 32, 1024)]
        else:
            pcs, outs = [(0, 512)], [(128 * s, 128, (g - 1) * 512 + 128 * s) for s in range(4)]
        fkeys = [("yc", i) for i in range(4)]
        for (a, b) in pcs:
            n = b - a
            bank, bk = self.pj()
            for c in range(NCH):
                t, tk = self.tmp()
                self.act_copy(t[:, 0:n], self.xT[:, c, a:b], [("x", c)], [tk], func=AF.Square)
                self.mm(bank[:, 0:n], self.ones_r[:], t[:, 0:n], c == 0, c == NCH - 1, [tk], [bk], sig=True)
            self.rsqrt_inplace(self.rstd[:, a:b], "rstd", bank[:, 0:n], bk)
        for (c0, n, row0) in outs:
            for c in range(NCH):
                self.stt(self.fin[:, c, 0:n], self.xT[:, c, c0:c0 + n], self.pvec[:, PV_FIN + c:PV_FIN + c + 1], self.rstd[:, c0:c0 + n],
                         ALU.mult, ALU.mult, [("x", c), "rstd"] + fkeys, fkeys)
            for c4 in range(4):
                bank, bk = self.pj()
                bv = bank[:].rearrange("p (j t) -> p j t", j=4)
                for j in range(4):
                    self.tr(bv[0:n, j, :], self.fin[:, c4 * 4 + j, 0:n], self.identf[:], fkeys, [bk], j == 3)
                self.act_copy(self.xs[0:n, c4 * 512:(c4 + 1) * 512], bank[0:n, :], [bk], ["xs"])
            self.dma_out(self.y[row0:row0 + n, :], self.xs[0:n, :], ["xs"])


_W_IN_ORDER = None


def _win_cols():
    chunks = []
    for c in range(8):
        chunks.append(np.arange(128 * c, 128 * c + 128))
    for g in range(4):
        cols = np.arange(1024 + 64 * g, 1024 + 64 * g + 64)
        chunks.append(np.concatenate([cols, cols]))
    for c in range(2):
        chunks.append(np.arange(1280 + 128 * c, 1280 + 128 * c + 128))
    for c in range(8):
        chunks.append(np.arange(1536 + 128 * c, 1536 + 128 * c + 128))
        chunks.append(np.arange(2560 + 128 * c, 2560 + 128 * c + 128))
    for c in range(16):
        chunks.append(np.arange(3584 + 128 * c, 3584 + 128 * c + 128))
        chunks.append(np.arange(5632 + 128 * c, 5632 + 128 * c + 128))
    return np.concatenate(chunks)


def _blockify(w, ncolblk=128):
    L, K, N = w.shape
    nk = K // 128
    nb = N // ncolblk
    r = w.reshape(L, nk, 128, nb, ncolblk).transpose(0, 3, 2, 1, 4)
    return np.ascontiguousarray(r).reshape(L, nb, 128, nk * ncolblk)


def _rope_tables(pos):
    half = 8
    inv = (np.float32(500000.0) ** (-np.arange(half, dtype=np.float32) * np.float32(2.0) / np.float32(16.0))).astype(np.float32)
    ang = pos.astype(np.float32)[None, :] * inv[:, None]
    cos = np.cos(ang).astype(np.float32)
    sin = np.sin(ang).astype(np.float32)
    T = pos.shape[0]
    C = np.ones((64, T), np.float32)
    Sn = np.zeros((64, T), np.float32)
    C[0:8] = cos
    C[8:16] = cos
    Sn[0:8] = sin
    Sn[8:16] = sin
    return np.concatenate([C, C], 0), np.concatenate([Sn, Sn], 0)


_PROG_CACHE = {}


RUN_DEPTH = DEPTH
RUN_CORES = list(range(8))
STOP = None
DBG = None


def _get_prog():
    if "nc" not in _PROG_CACHE:
        _PROG_CACHE["nc"] = Prog(depth=RUN_DEPTH).build()
    return _PROG_CACHE["nc"]


def kernel(x_prompt, x_sample, cache_k, cache_v, state_conv, meta_tokens, w_in, attn_sinks, w_attn_out,
           conv_dw, conv_dw_bias, conv_ln_g, conv_ln_b, w_conv_out, b_conv_out, w_o, norm_mix, norm_ffn,
           w_ffn_in, w_ffn_out, norm_final):
    f = lambda a: np.asarray(a, dtype=np.float32)
    x_prompt, x_sample, cache_k, cache_v, state_conv, meta_tokens = map(f, (x_prompt, x_sample, cache_k, cache_v, state_conv, meta_tokens))
    w_in, w_attn_out, w_conv_out, w_o, w_ffn_in, w_ffn_out = map(f, (w_in, w_attn_out, w_conv_out, w_o, w_ffn_in, w_ffn_out))
    L = DEPTH
    LW = RUN_DEPTH
    w_in, w_attn_out, w_conv_out, w_o, w_ffn_in, w_ffn_out = (a[:LW] for a in (w_in, w_attn_out, w_conv_out, w_o, w_ffn_in, w_ffn_out))
    win_r = _blockify(w_in[:, :, _win_cols()])
    wao_r = _blockify(w_attn_out)
    wco_r = _blockify(w_conv_out)
    wo_r = _blockify(w_o)
    f1cols = np.concatenate([np.concatenate([np.arange(128 * j, 128 * j + 128), np.arange(5632 + 128 * j, 5632 + 128 * j + 128)])
                             for j in range(44)])
    wf1_r = _blockify(w_ffn_in[:, :, f1cols])
    wf2_r = np.ascontiguousarray(w_ffn_out.reshape(LW, 4, 11, 128, 16, 128).transpose(0, 1, 4, 3, 2, 5)).reshape(LW, 4, 16, 128, 11 * 128)
    pvec = np.zeros((128, NV), np.float32)
    fm = lambda v, n: np.asarray(v, np.float32).reshape(n, 128).T
    for l in range(L):
        b = l * PV_L
        pvec[:, b:b + 16] = fm(norm_mix[l], 16)
        pvec[:, b + 16:b + 32] = fm(norm_ffn[l], 16)
        pvec[:, b + 32:b + 48] = fm(b_conv_out[l], 16)
        pvec[:, b + 48:b + 56] = fm(conv_dw_bias[l], 8)
        pvec[:, b + 56:b + 64] = fm(conv_ln_g[l], 8)
        pvec[:, b + 64:b + 72] = fm(conv_ln_b[l], 8)
        dw = np.asarray(conv_dw[l], np.float32)
        pvec[:, b + 72:b + 320] = dw.reshape(31, 8, 128).transpose(2, 1, 0).reshape(128, 248)
        pvec[:, PV_SINK + 16 * l:PV_SINK + 16 * l + 16] = np.asarray(attn_sinks[l], np.float32)[None, :]
    pvec[:, PV_FIN:PV_FIN + 16] = fm(norm_final, 16)
    identf = np.eye(128, dtype=np.float32)
    rmat = np.zeros((128, 128), np.float32)
    for base in (0, 64):
        for d in range(8):
            rmat[base + d + 8, base + d] = -1.0
            rmat[base + d, base + d + 8] = 1.0
    ii = np.arange(128)[:, None]
    jj = np.arange(128)[None, :]
    m_prev = np.where(jj >= ii, 0.0, NEG).astype(np.float32)
    m_own = np.where(jj <= ii, 0.0, NEG).astype(np.float32)
    m_std = np.concatenate([m_prev, m_own], 1)
    colv = (jj >= 112)
    m3 = np.concatenate([np.full((128, 128), NEG, np.float32), np.where((jj <= ii) & colv, 0.0, NEG).astype(np.float32)], 1)
    m4 = np.concatenate([np.where((jj >= ii) & colv, 0.0, NEG).astype(np.float32), m_own], 1)
    spos = 16384 + np.tile(np.arange(TS), NSEQ_S)
    shared = dict(win_r=win_r, wao_r=wao_r, wco_r=wco_r, wo_r=wo_r, wf1_r=wf1_r, wf2_r=wf2_r, pvec=pvec, identf=identf, rmat=rmat)
    in_maps = []
    for core in RUN_CORES:
        sq, ci = core // 4, core % 4
        xin = np.zeros((12 * 128 + 32, D), np.float32)
        pos = np.zeros((12 * 128,), np.int64)
        valid = np.ones((128, 512), np.float32)
        for s in range(12):
            tile = ci * 8 + (s - 4)
            if tile >= 0:
                xin[128 * s:128 * s + 128] = x_prompt[sq, 128 * tile:128 * tile + 128]
                pos[128 * s:128 * s + 128] = 16 + 128 * tile + np.arange(128)
            elif tile == -1:
                xin[128 * s + 112:128 * s + 128] = meta_tokens
                pos[128 * s + 112:128 * s + 128] = np.arange(16)
                valid[:, 128 * s:128 * s + 112] = 0.0
            else:
                valid[:, 128 * s:128 * s + 128] = 0.0
        xin[1536:1568] = x_sample[4 * core:4 * core + 4].reshape(32, D)
        rope = np.zeros((3, 2, 128, 544), np.float32)
        for g in range(3):
            p = pos[512 * g:512 * g + 512]
            p = np.concatenate([p, spos if g == 0 else np.zeros(32, np.int64)])
            rope[g, 0], rope[g, 1] = _rope_tables(p)
        masks = np.stack([m_std, m3 if ci == 0 else m_std, m4 if ci == 0 else m_std]).astype(np.float32)
        m = dict(shared)
        m.update(xin=xin, rope=rope, masks=masks, valid=valid,
                 ck=np.ascontiguousarray(cache_k[:, 4 * core:4 * core + 4].reshape(L, 4, 128, 256)),
                 cv=np.ascontiguousarray(cache_v[:, 4 * core:4 * core + 4].reshape(L, 4, 128, 256)),
                 sc=np.ascontiguousarray(state_conv[:, 4 * core:4 * core + 4]))
        in_maps.append(m)
    nc = _get_prog()
    res = run_bass_kernel_spmd(nc, in_maps, core_ids=list(range(len(RUN_CORES)))).results
    y_prompt = np.zeros((2, SEQ, D), np.float32)
    y_sample = np.zeros((32, TS, D), np.float32)
    nk_p = np.zeros((L, 2, 128, 4, 64), np.float32)
    nv_p = np.zeros((L, 2, 128, 4, 64), np.float32)
    nc_p = np.zeros((L, 2, 30, 1024), np.float32)
    nk_s = np.zeros((L, 32, 128, 4, 64), np.float32)
    nv_s = np.zeros((L, 32, 128, 4, 64), np.float32)
    nc_s = np.zeros((L, 32, 30, 1024), np.float32)
    for ri, core in enumerate(RUN_CORES):
        sq, ci = core // 4, core % 4
        r = res[ri]
        y_prompt[sq, 1024 * ci:1024 * ci + 1024] = r["y"][0:1024]
        y_sample[4 * core:4 * core + 4] = r["y"][1024:1056].reshape(4, TS, D)
        nk_s[:, 4 * core:4 * core + 4] = r["kso"].reshape(L, 4, 128, 4, 64)
        nv_s[:, 4 * core:4 * core + 4] = r["vso"].reshape(L, 4, 128, 4, 64)
        nc_s[:, 4 * core:4 * core + 4] = r["cso"]
        if ci == 3:
            nk_p[:, sq] = r["klast"].reshape(L, 128, 4, 64)
            nv_p[:, sq] = r["vlast"].reshape(L, 128, 4, 64)
            nc_p[:, sq] = r["clast"]
    return (y_prompt, y_sample, nk_p, nv_p, nc_p, nk_s, nv_s, nc_s)
```
